# Optimizing a Trainium2 kernel written in Bass

```python
import math
import jax, jax.numpy as jnp
from jax import lax
import numpy as np

D_MODEL = 2048
BATCH = 4
SEQ = 2048
DEPTH = 4

N_MIXERS = 2
N_GLA = (DEPTH + N_MIXERS - 1) // N_MIXERS
N_NA = DEPTH // N_MIXERS

GRID_W = 64

GLA_HEADS = 4
GLA_DK = D_MODEL // 2
GLA_DV = D_MODEL
GLA_DK_HEAD = GLA_DK // GLA_HEADS
GLA_DV_HEAD = GLA_DV // GLA_HEADS
GLA_GATE_RANK = 16
GLA_GATE_TEMP = 16.0
GLA_CHUNK = 64

NA_HEADS = 16
NA_HEAD_DIM = D_MODEL // NA_HEADS
NA_MAX_KR = 8
NA_KC = 16
NA_NCB = GRID_W // NA_KC

D_FF = 11 * D_MODEL // 4
CONV_W = 3

LN_EPS = 1e-5
RMS_EPS = 1e-6
NEG_INF = -1e9
DEEPNORM_ALPHA = (2.0 * DEPTH) ** 0.25
DEEPNORM_BETA = (8.0 * DEPTH) ** -0.25

kernel_name = "hybrid_gla_natten_convffn_deepnorm"


def _layer_norm(x, g, b):
    xf = x.astype(jnp.float32)
    mu = jnp.mean(xf, axis=-1, keepdims=True)
    xc = xf - mu
    var = jnp.mean(xc * xc, axis=-1, keepdims=True)
    return (xc * lax.rsqrt(var + LN_EPS) * g.astype(jnp.float32) + b.astype(jnp.float32)).astype(x.dtype)


def _gla_direction(q, k, v, log_a, strict):
    B, H, T, dk = q.shape
    dv = v.shape[-1]
    C = GLA_CHUNK
    N = T // C
    q = q.reshape(B, H, N, C, dk)
    k = k.reshape(B, H, N, C, dk)
    v = v.reshape(B, H, N, C, dv)
    b = jnp.cumsum(log_a.reshape(B, H, N, C, dk), axis=3)
    b_last = b[:, :, :, -1:, :]
    q_dec = q * jnp.exp(b)
    k_inv = k * jnp.exp(-b)
    k_state = k * jnp.exp(b_last - b)
    chunk_decay = jnp.exp(b_last[:, :, :, 0, :])
    mask = jnp.tril(jnp.ones((C, C), jnp.float32), k=-1 if strict else 0)
    scores = jnp.einsum('bhncd,bhnsd->bhncs', q_dec, k_inv) * mask
    o_intra = jnp.einsum('bhncs,bhnse->bhnce', scores, v)

    def step(S, inp):
        qd, ks, vv, dl = inp
        o = jnp.einsum('bhcd,bhde->bhce', qd, S)
        S = S * dl[..., None] + jnp.einsum('bhcd,bhce->bhde', ks, vv)
        return S, o

    xs = (jnp.moveaxis(q_dec, 2, 0), jnp.moveaxis(k_state, 2, 0),
          jnp.moveaxis(v, 2, 0), jnp.moveaxis(chunk_decay, 2, 0))
    S0 = jnp.zeros((B, H, dk, dv), jnp.float32)
    _, o_inter = lax.scan(step, S0, xs)
    o = o_intra + jnp.moveaxis(o_inter, 0, 2)
    return o.reshape(B, H, T, dv)


def _gla_mixer(x, w_in, w_gate_up, b_gate, norm_g, w_out):
    B, T, _ = x.shape
    p = x @ w_in
    q, k, v, g, lr = jnp.split(p, [GLA_DK, 2 * GLA_DK, 2 * GLA_DK + GLA_DV, 2 * GLA_DK + 2 * GLA_DV], axis=-1)
    lr = lr.reshape(B, T, 2, GLA_GATE_RANK)
    gate_logit = jnp.einsum('btzr,zrk->btzk', lr, w_gate_up) + b_gate
    log_a = jax.nn.log_sigmoid(gate_logit.astype(jnp.float32)) / GLA_GATE_TEMP
    log_a = log_a.reshape(B, T, 2, GLA_HEADS, GLA_DK_HEAD).transpose(2, 0, 3, 1, 4)

    def heads(t, hd):
        return t.astype(jnp.float32).reshape(B, T, GLA_HEADS, hd).transpose(0, 2, 1, 3)

    qh = heads(q, GLA_DK_HEAD) * (GLA_DK_HEAD ** -0.5)
    kh = heads(k, GLA_DK_HEAD)
    vh = heads(v, GLA_DV_HEAD)
    o_fwd = _gla_direction(qh, kh, vh, log_a[0], strict=False)
    flip = lambda t: jnp.flip(t, axis=2)
    o_bwd = flip(_gla_direction(flip(qh), flip(kh), flip(vh), flip(log_a[1]), strict=True))
    o = o_fwd + o_bwd
    o = o * lax.rsqrt(jnp.mean(o * o, axis=-1, keepdims=True) + RMS_EPS) * norm_g.astype(jnp.float32)
    o = o.transpose(0, 2, 1, 3).reshape(B, T, GLA_DV).astype(x.dtype)
    return (o * jax.nn.silu(g)) @ w_out


def _na_mixer(x, w_in, rpb, w_out):
    B, T, _ = x.shape
    rows = T // GRID_W
    kr = min(NA_MAX_KR, rows)
    hd = NA_HEAD_DIM
    qkv = (x @ w_in).astype(jnp.float32).reshape(B, rows, GRID_W, 3, NA_HEADS, hd)
    qkv = qkv.transpose(3, 0, 4, 1, 2, 5)
    q = qkv[0] * (hd ** -0.5)
    k = qkv[1]
    v = qkv[2]

    qcols = np.arange(GRID_W).reshape(NA_NCB, NA_KC)
    blk_start = np.clip(np.arange(NA_NCB) * NA_KC - NA_KC // 2, 0, GRID_W - 2 * NA_KC)
    key_cols = blk_start[:, None] + np.arange(2 * NA_KC)[None, :]
    win_c = np.clip(qcols - NA_KC // 2, 0, GRID_W - NA_KC)
    kc = key_cols[:, None, :]
    col_valid = (kc >= win_c[..., None]) & (kc < win_c[..., None] + NA_KC)
    dc_idx = np.clip(kc - qcols[..., None], -(NA_KC - 1), NA_KC - 1) + NA_KC - 1
    row_start = np.clip(np.arange(rows) - kr // 2, 0, rows - kr)
    dr_idx = row_start[:, None] + np.arange(kr)[None, :] - np.arange(rows)[:, None] + NA_MAX_KR - 1

    bias = rpb.astype(jnp.float32)[:, dr_idx]
    bias = bias[..., dc_idx]
    bias = bias.transpose(1, 0, 3, 4, 2, 5)
    bias = jnp.where(col_valid[None, None, :, :, None, :], bias, NEG_INF)

    def row_block(inp):
        q_r, rs, bias_r = inp
        k_band = lax.dynamic_slice_in_dim(k, rs, kr, axis=2)
        v_band = lax.dynamic_slice_in_dim(v, rs, kr, axis=2)
        k_blk = k_band[:, :, :, key_cols]
        v_blk = v_band[:, :, :, key_cols]
        q_blk = q_r.reshape(B, NA_HEADS, NA_NCB, NA_KC, hd)
        s = jnp.einsum('bhjqd,bhrjkd->bhjqrk', q_blk, k_blk) + bias_r
        pr = jax.nn.softmax(s, axis=(-2, -1))
        return jnp.einsum('bhjqrk,bhrjkd->bhjqd', pr, v_blk)

    q_rows = q.transpose(2, 0, 1, 3, 4)
    out = lax.map(row_block, (q_rows, jnp.asarray(row_start, jnp.int32), bias))
    out = out.reshape(rows, B, NA_HEADS, GRID_W, hd).transpose(1, 0, 3, 2, 4).reshape(B, T, D_MODEL)
    return out.astype(x.dtype) @ w_out


def _conv_ffn(x, w_up, conv_w, conv_b, w_down):
    h = x @ w_up
    hp = jnp.pad(h, ((0, 0), (1, 1), (0, 0)))
    h = hp[:, :-2] * conv_w[0] + hp[:, 1:-1] * conv_w[1] + hp[:, 2:] * conv_w[2] + conv_b
    a, b = jnp.split(h, 2, axis=-1)
    return (jax.nn.gelu(a, approximate=False) * b) @ w_down


def setup_inputs(seed: int = 0) -> dict:
    key = jax.random.key(seed)
    ks = jax.random.split(key, 17)
    f32 = jnp.float32

    def nrm(k, shape, s):
        return jax.random.normal(k, shape, f32) * s

    x = nrm(ks[0], (BATCH, SEQ, D_MODEL), 1.0)

    gla_cols = 2 * GLA_DK + 2 * GLA_DV + 2 * GLA_GATE_RANK
    col = jnp.arange(gla_cols)
    gla_scale = jnp.where((col >= 2 * GLA_DK) & (col < 2 * GLA_DK + GLA_DV), DEEPNORM_BETA, 1.0).astype(f32)
    gla_w_in = nrm(ks[1], (N_GLA, D_MODEL, gla_cols), D_MODEL ** -0.5) * gla_scale
    gla_w_gate_up = nrm(ks[2], (N_GLA, 2, GLA_GATE_RANK, GLA_DK), GLA_GATE_RANK ** -0.5)
    gla_b_gate = nrm(ks[3], (N_GLA, 2, GLA_DK), 0.01)
    gla_norm_g = 1.0 + nrm(ks[4], (N_GLA, GLA_DV_HEAD), 0.02)
    gla_w_out = nrm(ks[5], (N_GLA, GLA_DV, D_MODEL), GLA_DV ** -0.5 * DEEPNORM_BETA)

    na_col = jnp.arange(3 * D_MODEL)
    na_scale = jnp.where(na_col >= 2 * D_MODEL, DEEPNORM_BETA, 1.0).astype(f32)
    na_w_in = nrm(ks[6], (N_NA, D_MODEL, 3 * D_MODEL), D_MODEL ** -0.5) * na_scale
    na_rpb = nrm(ks[7], (N_NA, NA_HEADS, 2 * NA_MAX_KR - 1, 2 * NA_KC - 1), 0.05)
    na_w_out = nrm(ks[8], (N_NA, D_MODEL, D_MODEL), D_MODEL ** -0.5 * DEEPNORM_BETA)

    ffn_w_up = nrm(ks[9], (DEPTH, D_MODEL, 2 * D_FF), D_MODEL ** -0.5)
    ffn_conv_w = nrm(ks[10], (DEPTH, CONV_W, 2 * D_FF), CONV_W ** -0.5)
    ffn_conv_b = nrm(ks[11], (DEPTH, 2 * D_FF), 0.01)
    ffn_w_down = nrm(ks[12], (DEPTH, D_FF, D_MODEL), D_FF ** -0.5 * DEEPNORM_BETA)

    ln_mix_g = 1.0 + nrm(ks[13], (DEPTH, D_MODEL), 0.02)
    ln_mix_b = nrm(ks[14], (DEPTH, D_MODEL), 0.02)
    ln_ffn_g = 1.0 + nrm(ks[15], (DEPTH, D_MODEL), 0.02)
    ln_ffn_b = nrm(ks[16], (DEPTH, D_MODEL), 0.02)

    return {"x": x,
            "gla_w_in": gla_w_in, "gla_w_gate_up": gla_w_gate_up, "gla_b_gate": gla_b_gate,
            "gla_norm_g": gla_norm_g, "gla_w_out": gla_w_out,
            "na_w_in": na_w_in, "na_rpb": na_rpb, "na_w_out": na_w_out,
            "ffn_w_up": ffn_w_up, "ffn_conv_w": ffn_conv_w, "ffn_conv_b": ffn_conv_b, "ffn_w_down": ffn_w_down,
            "ln_mix_g": ln_mix_g, "ln_mix_b": ln_mix_b, "ln_ffn_g": ln_ffn_g, "ln_ffn_b": ln_ffn_b}


def reference(x, gla_w_in, gla_w_gate_up, gla_b_gate, gla_norm_g, gla_w_out,
              na_w_in, na_rpb, na_w_out,
              ffn_w_up, ffn_conv_w, ffn_conv_b, ffn_w_down,
              ln_mix_g, ln_mix_b, ln_ffn_g, ln_ffn_b):
    for i in range(DEPTH):
        j = i // N_MIXERS
        if i % N_MIXERS == 0:
            m = _gla_mixer(x, gla_w_in[j], gla_w_gate_up[j], gla_b_gate[j], gla_norm_g[j], gla_w_out[j])
        else:
            m = _na_mixer(x, na_w_in[j], na_rpb[j], na_w_out[j])
        x = _layer_norm(DEEPNORM_ALPHA * x + m, ln_mix_g[i], ln_mix_b[i])
        f = _conv_ffn(x, ffn_w_up[i], ffn_conv_w[i], ffn_conv_b[i], ffn_w_down[i])
        x = _layer_norm(DEEPNORM_ALPHA * x + f, ln_ffn_g[i], ln_ffn_b[i])
    return x
```

```python
import numpy as np
from contextlib import ExitStack
import concourse.bass as bass
import concourse.mybir as mybir
from concourse.bass_utils import run_bass_kernel_spmd

F32 = mybir.dt.float32
BF16 = mybir.dt.bfloat16
AF = mybir.ActivationFunctionType
ALU = mybir.AluOpType
AX = mybir.AxisListType

D = 2048
DC = 16
TL = 1024
XW = 1280
DEPTH = 4
FF = 5632
FC = 44
ALPHA = (2.0 * DEPTH) ** 0.25
LN_EPS = 1e-5
RMS_EPS = 1e-6
PAIRS = [[0, 1], [2, 3], [4, 5], [6, 7]]


class Eng:
    def __init__(self, K, name, e):
        self.K, self.name, self.e = K, name, e
        self.sem = K.new_sem("e_" + name)
        self.n = 0
        self.seen = {}

    def wait(self, tok):
        if tok is None:
            return
        sem, v = tok
        k = id(sem)
        if self.seen.get(k, 0) >= v:
            return
        self.e.wait_ge(sem, v)
        self.seen[k] = v

    def sig(self, ins):
        self.n += 1
        ins.then_inc(self.sem, 1)
        return (self.sem, self.n)


class Buf:
    def __init__(self, t=None):
        self.t = t
        self.w = None
        self.r = {}

    def __getitem__(self, idx):
        return self.t[idx]


class K:
    def __init__(self, nc):
        self.nc = nc
        self.es = ExitStack()
        self.nsem = 0
        self.pe = Eng(self, "pe", nc.tensor)
        self.dve = Eng(self, "dve", nc.vector)
        self.act = Eng(self, "act", nc.scalar)
        self.pool = Eng(self, "pool", nc.gpsimd)
        self.sp = Eng(self, "sp", nc.sync)
        self.engs = [self.pe, self.dve, self.act, self.pool, self.sp]
        self.dq = {}
        for q in (self.sp, self.pool, self.act):
            sems = [self.new_sem("d_%s%d" % (q.name, i)) for i in range(8)]
            self.dq[q.name] = {"sems": sems, "cnt": [0] * 8, "i": 0}
        self.uid = 0

    def new_sem(self, name):
        self.nsem += 1
        return self.es.enter_context(self.nc.semaphore(name))

    def name(self, p):
        self.uid += 1
        return "%s_%d" % (p, self.uid)

    def acc(self, eng, reads, writes):
        for b in reads:
            eng.wait(b.w)
        for b in writes:
            eng.wait(b.w)
            for t in list(b.r.values()):
                eng.wait(t)

    def done(self, tok, reads, writes):
        for b in reads:
            b.r[id(tok[0])] = tok
        for b in writes:
            b.w = tok
            b.r = {}

    def op(self, eng, fn, reads=(), writes=()):
        self.acc(eng, reads, writes)
        ins = fn(eng.e)
        tok = eng.sig(ins)
        self.done(tok, reads, writes)
        return tok

    def mm_group(self, fns, reads=(), writes=()):
        eng = self.pe
        self.acc(eng, reads, writes)
        ins = None
        for fn in fns:
            ins = fn(eng.e)
        tok = eng.sig(ins)
        self.done(tok, reads, writes)
        return tok

    def dma(self, q, out, in_, reads=(), writes=(), **kw):
        st = self.dq[q.name]
        i = st["i"]
        st["i"] = (i + 1) % len(st["sems"])
        sem = st["sems"][i]
        if st["cnt"][i] > 0:
            q.wait((sem, 16 * st["cnt"][i]))
        self.acc(q, reads, writes)
        ins = q.e.dma_start(out=out, in_=in_, **kw)
        st["cnt"][i] += 1
        ins.then_inc(sem, 16)
        tok = (sem, 16 * st["cnt"][i])
        self.done(tok, reads, writes)
        return tok

    def all_tokens(self):
        toks = [(e.sem, e.n) for e in self.engs if e.n > 0]
        for st in self.dq.values():
            for s, c in zip(st["sems"], st["cnt"]):
                if c > 0:
                    toks.append((s, 16 * c))
        return toks

    def barrier(self):
        toks = self.all_tokens()
        for e in self.engs:
            for t in toks:
                e.wait(t)

    def sb(self, st, shape, dt, name="t"):
        t = st.enter_context(self.nc.sbuf_tensor(self.name(name), list(shape), dt))
        return Buf(t)

    def ps(self, st, shape, dt=F32, name="ps"):
        t = st.enter_context(self.nc.psum_tensor(self.name(name), list(shape), dt))
        return Buf(t)


class Prog:
    def __init__(self, cfg):
        self.cfg = cfg
        nc = bass.Bass("TRN2", target_bir_lowering=False)
        self.nc = nc
        self.k = K(nc)
        self.dram_in = {}
        self.dram_bufs = {}

    def din(self, name, shape, dt=F32):
        t = self.nc.dram_tensor(name, list(shape), dt, kind="ExternalInput")
        b = Buf(t.ap())
        b.th = t
        self.dram_in[name] = b
        return b

    def dout(self, name, shape, dt=F32):
        t = self.nc.dram_tensor(name, list(shape), dt, kind="ExternalOutput")
        b = Buf(t.ap())
        b.th = t
        return b

    def dscr(self, name, shape, dt=F32):
        t = self.nc.dram_tensor(name, list(shape), dt)
        b = Buf(t.ap())
        b.th = t
        return b


def load_xT(P, st, xsrc, xT):
    k = P.k
    tmp = [k.sb(st, [128, 1024], F32, "ld") for _ in range(2)]
    for c in range(DC):
        t = tmp[c % 2]
        k.dma(k.sp, t[:, :], xsrc[c * 128:(c + 1) * 128, :], reads=[xsrc], writes=[t])
        eng = k.dve if c % 2 == 0 else k.pool
        k.op(eng, lambda e, t=t, c=c: e.tensor_copy(out=xT[:, c, 0:TL], in_=t[:, :]),
             reads=[t], writes=[xT])


def proj_resid(P, actT, KC, W, xres, ybuf, wname="w"):
    k = P.k
    with ExitStack() as st:
        KB = 11 if KC == 44 else 16
        NKB = KC // KB
        NW = 3
        wts = [k.sb(st, [128, KB, 256], BF16, "wres") for _ in range(NW)]
        pss = [k.ps(st, [128, 512], F32, "psr") for _ in range(8)]
        xr = [k.sb(st, [128, 512], F32, "xr") for _ in range(4)]
        yt = [k.sb(st, [128, 512], F32, "yt") for _ in range(4)]
        Wv = W.t.rearrange("(kc p) d -> p kc d", p=128)
        loads = [(p, kb) for p in range(8) for kb in range(NKB)]

        def issue(i):
            p, kb = loads[i]
            wt = wts[i % NW]
            k.dma(k.pool, wt[:, :, :], Wv[:, kb * KB:(kb + 1) * KB, p * 256:(p + 1) * 256],
                  reads=[W], writes=[wt])

        for i in range(min(NW - 1, len(loads))):
            issue(i)
        li = 0
        ei = 0
        for p in range(8):
            banks = pss[(p % 2) * 4:(p % 2) * 4 + 4]
            for kb in range(NKB):
                if li + NW - 1 < len(loads):
                    issue(li + NW - 1)
                wt = wts[li % NW]
                li += 1
                fns = []
                for dd in range(2):
                    for th in range(2):
                        ps = banks[dd * 2 + th]
                        for kk in range(KB):
                            kc = kb * KB + kk
                            fns.append(lambda e, ps=ps, wt=wt, kk=kk, kc=kc, dd=dd, th=th: e.matmul(
                                ps[:, :], wt[:, kk, dd * 128:(dd + 1) * 128],
                                actT[:, kc, th * 512:(th + 1) * 512],
                                start=(kc == 0), stop=(kc == KC - 1)))
                k.mm_group(fns, reads=[wt, actT], writes=banks)
            for dd in range(2):
                for th in range(2):
                    ps = banks[dd * 2 + th]
                    dch = p * 2 + dd
                    x_ = xr[ei % 4]
                    y_ = yt[ei % 4]
                    ei += 1
                    k.dma(k.sp, x_[:, :], xres[dch * 128:(dch + 1) * 128, th * 512:(th + 1) * 512],
                          reads=[xres], writes=[x_])
                    k.op(k.dve, lambda e, x_=x_, y_=y_, ps=ps: e.scalar_tensor_tensor(
                        out=y_[:, :], in0=x_[:, :], scalar=ALPHA, in1=ps[:, :],
                        op0=ALU.mult, op1=ALU.add), reads=[x_, ps], writes=[y_])
                    k.dma(k.sp, ybuf[dch * 128:(dch + 1) * 128, th * 512:(th + 1) * 512], y_[:, :],
                          reads=[y_], writes=[ybuf])
        k.barrier()


def ln_stage(P, ybuf, gb, li, xres_out, xT, consts, halo, xch):
    k = P.k
    with ExitStack() as st:
        ones = consts["ones"]
        ytl = k.sb(st, [128, DC, 512], F32, "ytl")
        sq = [k.sb(st, [128, 512], F32, "sq") for _ in range(2)]
        ps_s = k.ps(st, [128, 512], F32, "ps_s")
        ps_q = k.ps(st, [128, 512], F32, "ps_q")
        mean = k.sb(st, [128, 512], F32, "mean")
        rstd = k.sb(st, [128, 512], F32, "rstd")
        tmp = k.sb(st, [128, 512], F32, "tmp")
        t1 = [k.sb(st, [128, 512], F32, "t1") for _ in range(2)]
        xo = [k.sb(st, [128, 512], F32, "xo") for _ in range(3)]
        for th in range(2):
            ts = slice(th * 512, (th + 1) * 512)
            for c in range(DC):
                k.dma(k.sp, ytl[:, c, :], ybuf[c * 128:(c + 1) * 128, ts], reads=[ybuf], writes=[ytl])
            for c in range(DC):
                s_ = sq[c % 2]
                k.op(k.act, lambda e, s_=s_, c=c: e.activation(out=s_[:, :], in_=ytl[:, c, :], func=AF.Square),
                     reads=[ytl], writes=[s_])
                k.mm_group([lambda e, c=c: e.matmul(ps_s[:, :], ones[:, :], ytl[:, c, :],
                                                    start=(c == 0), stop=(c == DC - 1))],
                           reads=[ones, ytl], writes=[ps_s])
                k.mm_group([lambda e, c=c, s_=s_: e.matmul(ps_q[:, :], ones[:, :], s_[:, :],
                                                           start=(c == 0), stop=(c == DC - 1))],
                           reads=[ones, s_], writes=[ps_q])
            k.op(k.act, lambda e: e.activation(out=mean[:, :], in_=ps_s[:, :], func=AF.Copy, scale=1.0 / D),
                 reads=[ps_s], writes=[mean])
            k.op(k.act, lambda e: e.activation(out=tmp[:, :], in_=ps_q[:, :], func=AF.Copy, scale=1.0 / D),
                 reads=[ps_q], writes=[tmp])
            k.op(k.dve, lambda e: e.tensor_tensor(out=rstd[:, :], in0=mean[:, :], in1=mean[:, :], op=ALU.mult),
                 reads=[mean], writes=[rstd])
            k.op(k.dve, lambda e: e.tensor_tensor(out=tmp[:, :], in0=tmp[:, :], in1=rstd[:, :], op=ALU.subtract),
                 reads=[tmp, rstd], writes=[tmp])
            k.op(k.act, lambda e: e.activation(out=tmp[:, :], in_=tmp[:, :], func=AF.Sqrt, bias=consts["eps_ln"][:, 0:1]),
                 reads=[tmp], writes=[tmp])
            k.op(k.dve, lambda e: e.reciprocal(out=rstd[:, :], in_=tmp[:, :]), reads=[tmp], writes=[rstd])
            for c in range(DC):
                a = t1[c % 2]
                o = xo[c % 3]
                k.op(k.dve, lambda e, a=a, c=c: e.tensor_tensor(out=a[:, :], in0=ytl[:, c, :], in1=mean[:, :],
                                                                op=ALU.subtract), reads=[ytl, mean], writes=[a])
                k.op(k.dve, lambda e, a=a: e.tensor_tensor(out=a[:, :], in0=a[:, :], in1=rstd[:, :], op=ALU.mult),
                     reads=[a, rstd], writes=[a])
                gcol = (li * 2 + 0) * DC + c
                bcol = (li * 2 + 1) * DC + c
                k.op(k.act, lambda e, a=a, o=o, gcol=gcol, bcol=bcol: e.activation(
                    out=o[:, :], in_=a[:, :], func=AF.Identity, scale=gb[:, gcol:gcol + 1], bias=gb[:, bcol:bcol + 1]),
                    reads=[a, gb], writes=[o])
                k.op(k.pool, lambda e, o=o, c=c, ts=ts: e.tensor_copy(out=xT[:, c, ts], in_=o[:, :]),
                     reads=[o], writes=[xT])
                k.dma(k.sp, xres_out[c * 128:(c + 1) * 128, ts], o[:, :], reads=[o], writes=[xres_out])
        if halo is not None:
            exchange_halo(P, st, xT, halo, xch, consts)
        k.barrier()


def exchange_halo(P, st, xT, halo, xch, consts):
    k = P.k
    H = 64 if halo == "ffn" else 256
    snd, rcv = xch["snd"], xch["rcv"]
    flags = consts["flags"]
    for c in range(DC):
        k.dma(k.sp, snd[c * 128:(c + 1) * 128, 0:H], xT[:, c, TL - H:TL], reads=[xT], writes=[snd])
    k.acc(k.pool, [snd], [rcv])
    ins = k.nc.gpsimd.collective_compute("AllGather", ALU.bypass, replica_groups=PAIRS,
                                          ins=[snd.th.ap().opt()], outs=[rcv.th.ap().opt()])
    sem = xch["sem"]
    xch["cnt"] += 1
    ins.then_inc(sem)
    tok = (sem, xch["cnt"])
    k.done(tok, [snd], [rcv])
    r0 = k.sb(st, [128, DC, H], BF16, "r0")
    r1 = k.sb(st, [128, DC, H], BF16, "r1")
    rv = rcv.t.rearrange("(r c p) h -> r p c h", r=2, p=128)
    k.dma(k.sp, r0[:, :, :], rv[0, :, :, 0:H], reads=[rcv], writes=[r0])
    k.dma(k.sp, r1[:, :, :], rv[1, :, :, 0:H], reads=[rcv], writes=[r1])
    k.op(k.dve, lambda e: e.tensor_scalar(out=r0[:, :, :], in0=r0[:, :, :], scalar1=flags[:, 1:2], scalar2=None,
                                          op0=ALU.mult), reads=[r0, flags], writes=[r0])
    if halo == "ffn":
        k.op(k.dve, lambda e: e.scalar_tensor_tensor(out=xT[:, :, TL:TL + 1], in0=r1[:, :, 63:64], scalar=flags[:, 0:1],
                                                     in1=r0[:, :, 63:64], op0=ALU.mult, op1=ALU.add),
             reads=[r0, r1, flags], writes=[xT])
    else:
        for j in range(4):
            src = slice((3 - j) * 64, (4 - j) * 64)
            k.op(k.dve, lambda e, j=j, src=src: e.scalar_tensor_tensor(
                out=xT[:, :, TL + 64 * j:TL + 64 * (j + 1)], in0=r1[:, :, src], scalar=flags[:, 0:1],
                in1=r0[:, :, src], op0=ALU.mult, op1=ALU.add), reads=[r0, r1, flags], writes=[xT])


def ffn_stage(P, xT, wup, wdn, cv, layer, xres, ybuf):
    k = P.k
    NT = TL + 1
    splits = [(0, 342), (342, 342), (684, 341)]
    with ExitStack() as st:
        gT = k.sb(st, [128, FC, TL], BF16, "gT")
        with ExitStack() as st2:
            NW = 3
            wts = [k.sb(st2, [128, DC, 256], BF16, "wup") for _ in range(NW)]
            banks = [k.ps(st2, [128, 512], F32, "psu") for _ in range(6)]
            hs = [k.sb(st2, [128, NT], F32, "hs") for _ in range(2)]
            cab = [k.sb(st2, [128, TL], F32, "cab") for _ in range(4)]
            ga = [k.sb(st2, [128, TL], F32, "ga") for _ in range(2)]
            Wv = wup.t.rearrange("(kc p) f -> p kc f", p=128)
            NG = FC // 2
            loads = [(g, h) for g in range(NG) for h in range(2)]

            def issue(i):
                g, h = loads[i]
                col = h * FF + g * 256
                wt = wts[i % NW]
                k.dma(k.pool, wt[:, :, :], Wv[:, :, col:col + 256], reads=[wup], writes=[wt])

            for i in range(NW - 1):
                issue(i)
            ui = 0
            for i, (g, h) in enumerate(loads):
                if i + NW - 1 < len(loads):
                    issue(i + NW - 1)
                wt = wts[i % NW]
                for jj in range(2):
                    j = g * 2 + jj
                    ch = h * FC + j
                    bset = banks[(ui % 2) * 3:(ui % 2) * 3 + 3]
                    fns = []
                    for si, (s0, sn) in enumerate(splits):
                        for kc in range(DC):
                            fns.append(lambda e, ps=bset[si], kc=kc, s0=s0, sn=sn, jj=jj, wt=wt: e.matmul(
                                ps[:, 0:sn], wt[:, kc, jj * 128:(jj + 1) * 128], xT[:, kc, s0:s0 + sn],
                                start=(kc == 0), stop=(kc == DC - 1)))
                    k.mm_group(fns, reads=[wt, xT], writes=bset)
                    h_ = hs[ui % 2]
                    for si, (s0, sn) in enumerate(splits):
                        k.op(k.act, lambda e, h_=h_, ps=bset[si], s0=s0, sn=sn: e.activation(
                            out=h_[:, s0:s0 + sn], in_=ps[:, 0:sn], func=AF.Copy),
                            reads=[bset[si]], writes=[h_])
                    c_ = cab[(h * 2 + jj)]
                    pc = (layer * 88 + ch) * 4
                    k.op(k.act, lambda e, c_=c_, h_=h_, pc=pc: e.activation(
                        out=c_[:, :], in_=h_[:, 0:TL], func=AF.Identity, scale=cv[:, pc + 1:pc + 2],
                        bias=cv[:, pc + 3:pc + 4]), reads=[h_, cv], writes=[c_])
                    k.op(k.dve, lambda e, c_=c_, h_=h_, pc=pc: e.scalar_tensor_tensor(
                        out=c_[:, 1:TL], in0=h_[:, 0:TL - 1], scalar=cv[:, pc:pc + 1], in1=c_[:, 1:TL],
                        op0=ALU.mult, op1=ALU.add), reads=[h_, cv, c_], writes=[c_])
                    k.op(k.dve, lambda e, c_=c_, h_=h_, pc=pc: e.scalar_tensor_tensor(
                        out=c_[:, 0:TL], in0=h_[:, 1:TL + 1], scalar=cv[:, pc + 2:pc + 3], in1=c_[:, 0:TL],
                        op0=ALU.mult, op1=ALU.add), reads=[h_, cv, c_], writes=[c_])
                    ui += 1
                if h == 1:
                    for jj in range(2):
                        j = g * 2 + jj
                        g_ = ga[jj]
                        ca, cb = cab[jj], cab[2 + jj]
                        k.op(k.act, lambda e, g_=g_, ca=ca: e.activation(out=g_[:, :], in_=ca[:, :], func=AF.Gelu),
                             reads=[ca], writes=[g_])
                        k.op(k.dve, lambda e, g_=g_, cb=cb, j=j: e.tensor_tensor(
                            out=gT[:, j, :], in0=g_[:, :], in1=cb[:, :], op=ALU.mult),
                            reads=[g_, cb], writes=[gT])
            k.barrier()
        proj_resid(P, gT, FC, wdn, xres, ybuf)


class WStream:
    def __init__(self, P, st, W, n=3):
        self.k = P.k
        self.W = W
        self.Wv = W.t.rearrange("(kc p) f -> p kc f", p=128)
        self.tiles = [P.k.sb(st, [128, DC, 256], BF16, "wst") for _ in range(n)]
        self.i = 0

    def load(self, col, width=256):
        t = self.tiles[self.i % len(self.tiles)]
        self.i += 1
        self.k.dma(self.k.pool, t[:, :, 0:width], self.Wv[:, :, col:col + width], reads=[self.W], writes=[t])
        return t


def gla_stage(P, xT, j, xres, ybuf, consts):
    k = P.k
    win, wout = P.wfull["gla_w_in_%d" % j], P.wfull["gla_w_out_%d" % j]
    gp = P.gla_params[j]
    o1 = P.gla_o1
    ssnd, srcv = P.gla_snd, P.gla_rcv
    flags = consts["flags"]
    with ExitStack() as st:
        ogT = k.sb(st, [128, DC, TL], BF16, "ogT")
        with ExitStack() as s2:
            ws = WStream(P, s2, win, 2)
            wg = k.sb(s2, [16, 2, 1024], F32, "wg")
            k.dma(k.sp, wg[:, :, :], gp["wg"].t, writes=[wg])
            bg = k.sb(s2, [128, 16], F32, "bg")
            k.dma(k.sp, bg[:, :], gp["bg"].t, writes=[bg])
            nbg = k.sb(s2, [128, 16], F32, "nbg")
            k.op(k.dve, lambda e: e.tensor_scalar(out=nbg[:, :], in0=bg[:, :], scalar1=-1.0, scalar2=None, op0=ALU.mult),
                 reads=[bg], writes=[nbg])
            ng = k.sb(s2, [128, 512], F32, "ng")
            k.dma(k.sp, ng[:, :], gp["ng"].t, writes=[ng])
            masks = k.sb(s2, [128, 2, 128], F32, "masks")
            k.dma(k.sp, masks[:, :, :], P.dram_in["gla_masks"].t, writes=[masks])
            ident = consts["ident_bf"]
            onesf = consts["ones"]
            lrp = [k.sb(s2, [16, TL], F32, "lrp") for _ in range(2)]
            psA = [k.ps(s2, [128, 512], F32, "psA") for _ in range(2)]
            ps_lr = psA
            s3 = ExitStack()
            lr = [k.sb(s3, [16, TL], F32, "lr") for _ in range(2)]
            wt = ws.load(6144, 32)
            for z in range(2):
                for th in range(2):
                    ps = ps_lr[th]
                    k.mm_group([lambda e, ps=ps, kc=kc, z=z, th=th: e.matmul(
                        ps[0:16, :], wt[:, kc, z * 16:(z + 1) * 16], xT[:, kc, th * 512:(th + 1) * 512],
                        start=(kc == 0), stop=(kc == DC - 1)) for kc in range(DC)], reads=[wt, xT], writes=[ps])
                    k.op(k.act, lambda e, ps=ps, z=z, th=th: e.activation(
                        out=lr[z][:, th * 512:(th + 1) * 512], in_=ps[0:16, :], func=AF.Copy),
                        reads=[ps], writes=[lr[z]])
            for p in range(2):
                k.op(k.dve, lambda e, p=p: e.tensor_scalar(out=lrp[p][:, :], in0=lr[1 - p][:, :], scalar1=flags[0:16, 1:2],
                                                         scalar2=None, op0=ALU.mult), reads=[lr[1 - p], flags], writes=[lrp[p]])
                k.op(k.dve, lambda e, p=p: e.scalar_tensor_tensor(out=lrp[p][:, :], in0=lr[p][:, :], scalar=flags[0:16, 0:1],
                                                                in1=lrp[p][:, :], op0=ALU.mult, op1=ALU.add),
                     reads=[lr[p], flags, lrp[p]], writes=[lrp[p]])
            k.barrier()
            s3.close()
            lt = k.sb(s2, [128, 2, TL], F32, "lt")
            cs = k.sb(s2, [128, 2, TL], F32, "cs")
            ub = k.sb(s2, [128, 2, 2, 8], F32, "ub")
            dec = k.sb(s2, [128, 2, 8], F32, "dec")
            E = [k.sb(s2, [128, 512], F32, "E") for _ in range(3)]
            qd = k.sb(s2, [128, 2, TL], BF16, "qd")
            ki = k.sb(s2, [128, 2, TL], BF16, "ki")
            kst = k.sb(s2, [128, 2, TL], BF16, "kst")
            kstm = k.sb(s2, [128, 8, 256], BF16, "kstm")
            vt = k.sb(s2, [128, 8, 512], BF16, "vt")
            sg = k.sb(s2, [128, 8, 512], BF16, "sg")
            S32 = k.sb(s2, [128, 2, 512], F32, "S32")
            Sbf = k.sb(s2, [128, 2, 512], BF16, "Sbf")
            r0 = k.sb(s2, [128, 512], F32, "sr0")
            r1 = k.sb(s2, [128, 512], F32, "sr1")
            msk = [k.sb(s2, [128, 128], BF16, "msk") for _ in range(2)]
            ot = [k.sb(s2, [128, 512], F32, "ot") for _ in range(2)]
            o1t = [k.sb(s2, [128, 512], F32, "o1t") for _ in range(2)]
            og = [k.sb(s2, [128, 512], BF16, "og") for _ in range(2)]
            sm = k.sb(s2, [128, 4], F32, "sm")
            junk = k.sb(s2, [128, 512], F32, "junk")
            psS = k.ps(s2, [128, 512], F32, "psS")
            psO = k.ps(s2, [128, 512], F32, "psO")
            psU = [k.ps(s2, [128, 512], F32, "psU") for _ in range(2)]
            psT = k.ps(s2, [128, 1024], BF16, "psT")
            ai = 0

            for p in range(2):
                if p == 1:
                    k.acc(k.pool, [ssnd], [srcv])
                    ins = k.nc.gpsimd.collective_compute("AllGather", ALU.bypass, replica_groups=PAIRS,
                                                          ins=[ssnd.th.ap().opt()], outs=[srcv.th.ap().opt()])
                    P.xch["cnt"] += 1
                    ins.then_inc(P.xch["sem"])
                    k.done((P.xch["sem"], P.xch["cnt"]), [ssnd], [srcv])
                for h in range(4):
                    for q in range(2):
                        ch = h * 2 + q
                        for th in range(2):
                            ps = psA[ai % 2]
                            ai += 1
                            k.mm_group([lambda e, ps=ps, p=p, ch=ch, th=th: e.matmul(
                                ps[:, :], wg[:, p, ch * 128:(ch + 1) * 128], lrp[p][:, th * 512:(th + 1) * 512],
                                start=True, stop=True)], reads=[wg, lrp[p]], writes=[ps])
                            k.op(k.act, lambda e, ps=ps, q=q, th=th, p=p, ch=ch: e.activation(
                                out=lt[:, q, th * 512:(th + 1) * 512], in_=ps[:, :], func=AF.Exp, scale=-1.0,
                                bias=nbg[:, p * 8 + ch:p * 8 + ch + 1]), reads=[ps, nbg], writes=[lt])
                        k.op(k.act, lambda e, q=q: e.activation(out=lt[:, q, :], in_=lt[:, q, :], func=AF.Ln, bias=1.0),
                             reads=[lt], writes=[lt])
                        for c in range(8):
                            k.op(k.dve, lambda e, q=q, c=c: e.tensor_tensor_scan(
                                out=cs[:, q, c * 128:(c + 1) * 128], data0=onesf[:, 0:128], data1=lt[:, q, c * 128:(c + 1) * 128],
                                initial=0.0, op0=ALU.mult, op1=ALU.add), reads=[onesf, lt], writes=[cs])
                        csl = cs[:, q, :].rearrange("p (c t) -> p c t", t=128)[:, :, 127]
                        k.op(k.dve, lambda e, q=q, csl=csl: e.tensor_scalar(out=ub[:, q, 0, :], in0=csl, scalar1=-1.0 / 16.0,
                                                                          scalar2=None, op0=ALU.mult), reads=[cs], writes=[ub])
                        k.op(k.dve, lambda e, q=q, csl=csl: e.tensor_scalar(out=ub[:, q, 1, :], in0=csl, scalar1=1.0 / 16.0,
                                                                          scalar2=None, op0=ALU.mult), reads=[cs], writes=[ub])
                        k.op(k.act, lambda e, q=q: e.activation(out=dec[:, q, :], in_=ub[:, q, 0, :], func=AF.Exp),
                             reads=[ub], writes=[dec])
                        if p == 1:
                            k.op(k.dve, lambda e, q=q: e.tensor_tensor(out=cs[:, q, :], in0=lt[:, q, :], in1=cs[:, q, :],
                                                                      op=ALU.subtract), reads=[lt, cs], writes=[cs])
                    wq = ws.load(h * 256)
                    wk = ws.load(1024 + h * 256)
                    for q in range(2):
                        for th in range(2):
                            for cc in range(4):
                                c = th * 4 + cc
                                tsl = slice(c * 128, (c + 1) * 128)
                                esl = slice(cc * 128, (cc + 1) * 128)
                                if p == 0:
                                    bq, bk, bs_ = 0.0, 0.0, ub[:, q, 0, c:c + 1]
                                else:
                                    bq, bk, bs_ = ub[:, q, 0, c:c + 1], ub[:, q, 1, c:c + 1], 0.0
                                k.op(k.act, lambda e, q=q, tsl=tsl, esl=esl, bq=bq: e.activation(
                                    out=E[0][:, esl], in_=cs[:, q, tsl], func=AF.Exp, scale=-1.0 / 16.0, bias=bq),
                                    reads=[cs, ub], writes=[E[0]])
                                k.op(k.act, lambda e, q=q, tsl=tsl, esl=esl, bk=bk: e.activation(
                                    out=E[1][:, esl], in_=cs[:, q, tsl], func=AF.Exp, scale=1.0 / 16.0, bias=bk),
                                    reads=[cs, ub], writes=[E[1]])
                                k.op(k.act, lambda e, q=q, tsl=tsl, esl=esl, bs_=bs_: e.activation(
                                    out=E[2][:, esl], in_=cs[:, q, tsl], func=AF.Exp, scale=1.0 / 16.0, bias=bs_),
                                    reads=[cs, ub], writes=[E[2]])
                            tsl = slice(th * 512, (th + 1) * 512)
                            ps = psA[ai % 2]
                            ai += 1
                            k.mm_group([lambda e, ps=ps, kc=kc, q=q, tsl=tsl: e.matmul(
                                ps[:, :], wq[:, kc, q * 128:(q + 1) * 128], xT[:, kc, tsl],
                                start=(kc == 0), stop=(kc == DC - 1)) for kc in range(DC)], reads=[wq, xT], writes=[ps])
                            k.op(k.dve, lambda e, ps=ps, q=q, tsl=tsl: e.scalar_tensor_tensor(
                                out=qd[:, q, tsl], in0=ps[:, :], scalar=1.0 / 16.0, in1=E[0][:, :],
                                op0=ALU.mult, op1=ALU.mult), reads=[ps, E[0]], writes=[qd])
                            ps = psA[ai % 2]
                            ai += 1
                            k.mm_group([lambda e, ps=ps, kc=kc, q=q, tsl=tsl: e.matmul(
                                ps[:, :], wk[:, kc, q * 128:(q + 1) * 128], xT[:, kc, tsl],
                                start=(kc == 0), stop=(kc == DC - 1)) for kc in range(DC)], reads=[wk, xT], writes=[ps])
                            k.op(k.dve, lambda e, ps=ps, q=q, tsl=tsl: e.tensor_tensor(
                                out=ki[:, q, tsl], in0=ps[:, :], in1=E[1][:, :], op=ALU.mult),
                                reads=[ps, E[1]], writes=[ki])
                            k.op(k.dve, lambda e, ps=ps, q=q, tsl=tsl: e.tensor_tensor(
                                out=kst[:, q, tsl], in0=ps[:, :], in1=E[2][:, :], op=ALU.mult),
                                reads=[ps, E[2]], writes=[kst])
                    for half in range(2):
                        fns = []
                        for cc in range(4):
                            c = half * 4 + cc
                            for q in range(2):
                                fns.append(lambda e, c=c, cc=cc, q=q: e.transpose(
                                    psT[:, cc * 256 + q * 128:cc * 256 + (q + 1) * 128], kst[:, q, c * 128:(c + 1) * 128],
                                    ident[:, :]))
                        k.mm_group(fns, reads=[kst, ident], writes=[psT])
                        k.op(k.act, lambda e, half=half: e.activation(
                            out=kstm[:, half * 4:(half + 1) * 4, :].rearrange("p a b -> p (a b)"), in_=psT[:, :], func=AF.Copy),
                            reads=[psT], writes=[kstm])
                    for which in range(2 if p == 1 else 1):
                        for hh in range(2):
                            wv = ws.load(2048 + which * 2048 + h * 512 + hh * 256)
                            for c in range(8):
                                ps = psA[ai % 2]
                                ai += 1
                                k.mm_group([lambda e, ps=ps, kc=kc, c=c: e.matmul(
                                    ps[:, 0:256], xT[:, kc, c * 128:(c + 1) * 128], wv[:, kc, :],
                                    start=(kc == 0), stop=(kc == DC - 1)) for kc in range(DC)], reads=[wv, xT], writes=[ps])
                                if which == 0:
                                    k.op(k.act, lambda e, ps=ps, c=c, hh=hh: e.activation(
                                        out=vt[:, c, hh * 256:(hh + 1) * 256], in_=ps[:, 0:256], func=AF.Copy),
                                        reads=[ps], writes=[vt])
                                else:
                                    k.op(k.act, lambda e, ps=ps, c=c, hh=hh: e.activation(
                                        out=sg[:, c, hh * 256:(hh + 1) * 256], in_=ps[:, 0:256], func=AF.Silu),
                                        reads=[ps], writes=[sg])
                    if p == 0:
                        k.op(k.pool, lambda e: e.memset(S32[:, :, :], 0.0), writes=[S32])
                    else:
                        rv = srcv.t.rearrange("(r h q p) f -> r h p q f", r=2, h=4, q=2)
                        for q in range(2):
                            k.dma(k.sp, r0[:, :], rv[0, h, :, q, :], reads=[srcv], writes=[r0])
                            k.dma(k.sp, r1[:, :], rv[1, h, :, q, :], reads=[srcv], writes=[r1])
                            k.op(k.dve, lambda e: e.tensor_scalar(out=r0[:, :], in0=r0[:, :], scalar1=flags[:, 1:2],
                                                                  scalar2=None, op0=ALU.mult), reads=[r0, flags], writes=[r0])
                            k.op(k.dve, lambda e, q=q: e.scalar_tensor_tensor(out=S32[:, q, :], in0=r1[:, :], scalar=flags[:, 0:1],
                                                                              in1=r0[:, :], op0=ALU.mult, op1=ALU.add),
                                 reads=[r0, r1, flags], writes=[S32])
                        k.op(k.act, lambda e: e.activation(out=Sbf[:, :, :], in_=S32[:, :, :], func=AF.Copy),
                             reads=[S32], writes=[Sbf])
                    order = list(range(8)) if p == 0 else list(range(7, -1, -1))
                    for ci, c in enumerate(order):
                        tsl = slice(c * 128, (c + 1) * 128)
                        k.mm_group([lambda e, q=q, tsl=tsl: e.matmul(psS[:, 0:128], ki[:, q, tsl], qd[:, q, tsl],
                                                                      start=(q == 0), stop=(q == 1)) for q in range(2)],
                                   reads=[ki, qd], writes=[psS])
                        m_ = msk[ci % 2]
                        k.op(k.dve, lambda e, m_=m_, p=p: e.tensor_tensor(out=m_[:, :], in0=psS[:, 0:128], in1=masks[:, p, :],
                                                                          op=ALU.mult), reads=[psS, masks], writes=[m_])
                        has_state = not (p == 0 and ci == 0)
                        fns = [lambda e, m_=m_, c=c: e.matmul(psO[:, :], m_[:, :], vt[:, c, :], start=True, stop=not has_state)]
                        if has_state:
                            for q in range(2):
                                fns.append(lambda e, q=q, tsl=tsl: e.matmul(psO[:, :], qd[:, q, tsl], Sbf[:, q, :],
                                                                            start=False, stop=(q == 1)))
                        k.mm_group(fns, reads=[m_, vt, qd, Sbf], writes=[psO])
                        for q in range(2):
                            k.mm_group([lambda e, q=q, c=c: e.matmul(psU[q][:, :], kstm[:, c, q * 128:(q + 1) * 128], vt[:, c, :],
                                                                      start=True, stop=True)],
                                       reads=[kstm, vt], writes=[psU[q]])
                            k.op(k.dve, lambda e, q=q, c=c: e.scalar_tensor_tensor(
                                out=S32[:, q, :], in0=S32[:, q, :], scalar=dec[:, q, c:c + 1], in1=psU[q][:, :],
                                op0=ALU.mult, op1=ALU.add), reads=[S32, dec, psU[q]], writes=[S32])
                        k.op(k.act, lambda e: e.activation(out=Sbf[:, :, :], in_=S32[:, :, :], func=AF.Copy),
                             reads=[S32], writes=[Sbf])
                        if p == 0:
                            o_ = ot[ci % 2]
                            k.op(k.act, lambda e, o_=o_: e.activation(out=o_[:, :], in_=psO[:, :], func=AF.Copy),
                                 reads=[psO], writes=[o_])
                            k.dma(k.sp, o1.t[c * 128:(c + 1) * 128, h * 512:(h + 1) * 512], o_[:, :], reads=[o_], writes=[o1])
                        else:
                            a_ = o1t[ci % 2]
                            o_ = ot[ci % 2]
                            g_ = og[ci % 2]
                            k.dma(k.sp, a_[:, :], o1.t[c * 128:(c + 1) * 128, h * 512:(h + 1) * 512], reads=[o1], writes=[a_])
                            k.op(k.dve, lambda e, a_=a_, o_=o_: e.tensor_tensor(out=o_[:, :], in0=a_[:, :], in1=psO[:, :], op=ALU.add),
                                 reads=[a_, psO], writes=[o_])
                            k.op(k.act, lambda e, o_=o_: e.activation(out=junk[:, :], in_=o_[:, :], func=AF.Square,
                                                                      accum_out=sm[:, 0:1]), reads=[o_], writes=[junk, sm])
                            k.op(k.dve, lambda e: e.tensor_scalar(out=sm[:, 1:2], in0=sm[:, 0:1], scalar1=1.0 / 512.0,
                                                                  scalar2=RMS_EPS, op0=ALU.mult, op1=ALU.add), reads=[sm], writes=[sm])
                            k.op(k.act, lambda e: e.activation(out=sm[:, 2:3], in_=sm[:, 1:2], func=AF.Sqrt), reads=[sm], writes=[sm])
                            k.op(k.dve, lambda e: e.reciprocal(out=sm[:, 3:4], in_=sm[:, 2:3]), reads=[sm], writes=[sm])
                            k.op(k.dve, lambda e, o_=o_: e.scalar_tensor_tensor(out=o_[:, :], in0=o_[:, :], scalar=sm[:, 3:4], in1=ng[:, :],
                                                                              op0=ALU.mult, op1=ALU.mult), reads=[o_, sm, ng], writes=[o_])
                            k.op(k.dve, lambda e, o_=o_, g_=g_, c=c: e.tensor_tensor(out=g_[:, :], in0=o_[:, :], in1=sg[:, c, :], op=ALU.mult),
                                 reads=[o_, sg], writes=[g_])
                            k.mm_group([lambda e, g_=g_, i=i: e.transpose(psT[:, i * 128:(i + 1) * 128], g_[:, i * 128:(i + 1) * 128], ident[:, :])
                                        for i in range(4)], reads=[g_, ident], writes=[psT])
                            k.op(k.act, lambda e, h=h, tsl=tsl: e.activation(
                                out=ogT[:, h * 4:(h + 1) * 4, tsl], in_=psT[:, 0:512].rearrange("p (a b) -> p a b", a=4), func=AF.Copy),
                                reads=[psT], writes=[ogT])
                    if p == 0:
                        sv = ssnd.t.rearrange("(h q p) f -> h p q f", h=4, q=2)
                        k.dma(k.sp, sv[h], S32[:, :, :], reads=[S32], writes=[ssnd])
            k.barrier()
        proj_resid(P, ogT, DC, wout, xres, ybuf)


def na_stage(P, xT, j, xres, ybuf, consts):
    k = P.k
    win, wout = P.wfull["na_w_in_%d" % j], P.wfull["na_w_out_%d" % j]
    tb = P.na_params[j]["tb"]
    ident = consts["ident_bf"]
    KS = [(0, 512), (512, 512), (1024, 256)]
    NM = P.cfg.get('na_rows', 8)
    with ExitStack() as st:
        OT = k.sb(st, [128, DC, TL], BF16, "OT")
        with ExitStack() as s2:
            Wv = win.t.rearrange("(kc p) f -> p kc f", p=128)
            wts = [k.sb(s2, [128, DC, 128], BF16, "naw") for _ in range(6)]
            qT = [k.sb(s2, [128, TL], BF16, "qT") for _ in range(2)]
            kT = [k.sb(s2, [128, XW], BF16, "kT") for _ in range(2)]
            V = [k.sb(s2, [128, 10, 128], BF16, "V") for _ in range(2)]
            bt = [k.sb(s2, [128, 640], F32, "bt") for _ in range(2)]
            sbt = [k.sb(s2, [128, 640], F32, "sbt") for _ in range(2)]
            Pt = [k.sb(s2, [128, 640], BF16, "Pt") for _ in range(3)]
            PT = [k.sb(s2, [128, 5, 128], BF16, "PT") for _ in range(3)]
            Osb = [k.sb(s2, [128, 128], BF16, "Osb") for _ in range(2)]
            sm = [k.sb(s2, [128, 4], F32, "sm") for _ in range(3)]
            psS = [k.ps(s2, [128, 512], F32, "psS") for _ in range(4)]
            psT = [k.ps(s2, [128, 1024], BF16, "psT") for _ in range(2)]
            psT2 = [k.ps(s2, [128, 1024], BF16, "psT2")] * 2
            psO = [k.ps(s2, [128, 512], F32, "psO")] * 2
            state = {"ai": 0}

            def proj_chunks(h):
                hb = h % 2
                w3 = [wts[(h % 2) * 3 + i] for i in range(3)]
                chunks = []

                def ld():
                    for i in range(3):
                        col = i * 2048 + h * 128
                        k.dma(k.pool, w3[i][:, :, :], Wv[:, :, col:col + 128], reads=[win], writes=[w3[i]])

                def cq(th):
                    def f():
                        if th == 0:
                            ld()
                        ps = psS[state["ai"] % 4]
                        state["ai"] += 1
                        k.mm_group([lambda e, ps=ps, kc=kc: e.matmul(
                            ps[:, :], w3[0][:, kc, :], xT[:, kc, th * 512:(th + 1) * 512],
                            start=(kc == 0), stop=(kc == DC - 1)) for kc in range(DC)], reads=[w3[0], xT], writes=[ps])
                        k.op(k.act, lambda e, ps=ps: e.activation(out=qT[hb][:, th * 512:(th + 1) * 512], in_=ps[:, :],
                                                                   func=AF.Copy, scale=128.0 ** -0.5), reads=[ps], writes=[qT[hb]])
                    return f

                def ck(s0, sn):
                    def f():
                        ps = psS[state["ai"] % 4]
                        state["ai"] += 1
                        k.mm_group([lambda e, ps=ps, kc=kc: e.matmul(
                            ps[:, 0:sn], w3[1][:, kc, :], xT[:, kc, s0:s0 + sn],
                            start=(kc == 0), stop=(kc == DC - 1)) for kc in range(DC)], reads=[w3[1], xT], writes=[ps])
                        k.op(k.act, lambda e, ps=ps: e.activation(out=kT[hb][:, s0:s0 + sn], in_=ps[:, 0:sn], func=AF.Copy),
                             reads=[ps], writes=[kT[hb]])
                    return f

                def cv(g4):
                    def f():
                        n = min(4, 10 - g4)
                        ps = psS[state["ai"] % 4]
                        state["ai"] += 1
                        fns = []
                        for i in range(n):
                            pt = g4 + i
                            for kc in range(DC):
                                fns.append(lambda e, ps=ps, kc=kc, pt=pt, i=i: e.matmul(
                                    ps[:, i * 128:(i + 1) * 128], xT[:, kc, pt * 128:(pt + 1) * 128], w3[2][:, kc, :],
                                    start=(kc == 0), stop=(kc == DC - 1)))
                        k.mm_group(fns, reads=[w3[2], xT], writes=[ps])
                        k.op(k.act, lambda e, ps=ps: e.activation(
                            out=V[hb][:, g4:g4 + n, :].rearrange("p a b -> p (a b)"), in_=ps[:, 0:n * 128], func=AF.Copy),
                            reads=[ps], writes=[V[hb]])
                    return f

                chunks = [cq(0), cq(1)] + [ck(s0, sn) for (s0, sn) in KS] + [cv(0), cv(4), cv(8)]
                return chunks

            def phaseA(i, h, m):
                hb = h % 2
                p0 = min(max(m - 2, 0), 5)
                k0 = p0 * 128
                b_, s_, p_, m_ = bt[i % 2], sbt[i % 2], Pt[i % 3], sm[i % 3]
                qs = slice(m * 128, (m + 1) * 128)
                k.dma(k.sp, b_[:, :], tb.t[h, m, :, :], reads=[tb], writes=[b_])
                for jj in range(2):
                    ps = psS[(i % 2) * 2 + jj]
                    k.mm_group([lambda e, jj=jj, ps=ps: e.matmul(
                        ps[:, 0:320], qT[hb][:, qs], kT[hb][:, k0 + jj * 320:k0 + (jj + 1) * 320],
                        start=True, stop=True)], reads=[qT[hb], kT[hb]], writes=[ps])
                    k.op(k.dve, lambda e, jj=jj, ps=ps: e.tensor_tensor(
                        out=s_[:, jj * 320:(jj + 1) * 320], in0=ps[:, 0:320], in1=b_[:, jj * 320:(jj + 1) * 320],
                        op=ALU.add), reads=[ps, b_], writes=[s_])
                k.op(k.dve, lambda e: e.tensor_reduce(out=m_[:, 3:4], in_=s_[:, :], axis=AX.X, op=ALU.max),
                     reads=[s_], writes=[m_])
                k.op(k.dve, lambda e: e.tensor_scalar(out=m_[:, 0:1], in0=m_[:, 3:4], scalar1=-1.0, scalar2=None, op0=ALU.mult),
                     reads=[m_], writes=[m_])
                k.op(k.act, lambda e: e.activation(out=p_[:, :], in_=s_[:, :], func=AF.Exp, bias=m_[:, 0:1],
                                                   accum_out=m_[:, 1:2]), reads=[s_, m_], writes=[p_, m_])
                k.op(k.dve, lambda e: e.reciprocal(out=m_[:, 2:3], in_=m_[:, 1:2]), reads=[m_], writes=[m_])

            def phaseB(i, h, m):
                p_, pT_, pst = Pt[i % 3], PT[i % 3], psT[i % 2]
                k.mm_group([lambda e, ii=ii: e.transpose(pst[:, ii * 128:(ii + 1) * 128], p_[:, ii * 128:(ii + 1) * 128], ident[:, :])
                            for ii in range(5)], reads=[p_, ident], writes=[pst])
                k.op(k.act, lambda e: e.activation(out=pT_[:, :, :].rearrange("p a b -> p (a b)"), in_=pst[:, 0:640],
                                                   func=AF.Copy), reads=[pst], writes=[pT_])

            def phaseC(i, h, m):
                hb = h % 2
                p0 = min(max(m - 2, 0), 5)
                pT_, o_, m_, pso, pt2 = PT[i % 3], Osb[i % 2], sm[i % 3], psO[i % 2], psT2[i % 2]
                qs = slice(m * 128, (m + 1) * 128)
                k.mm_group([lambda e, ii=ii: e.matmul(pso[:, 0:128], pT_[:, ii, :], V[hb][:, p0 + ii, :],
                                                      start=(ii == 0), stop=(ii == 4)) for ii in range(5)],
                           reads=[pT_, V[hb]], writes=[pso])
                k.op(k.dve, lambda e: e.tensor_scalar(out=o_[:, :], in0=pso[:, 0:128], scalar1=m_[:, 2:3], scalar2=None,
                                                      op0=ALU.mult), reads=[pso, m_], writes=[o_])
                k.mm_group([lambda e: e.transpose(pt2[:, 0:128], o_[:, :], ident[:, :])],
                           reads=[o_, ident], writes=[pt2])
                k.op(k.dve, lambda e: e.tensor_copy(out=OT[:, h, qs], in_=pt2[:, 0:128]),
                     reads=[pt2], writes=[OT])

            items = [(h, m) for h in range(16) for m in range(NM)]
            pending = {}
            for f in proj_chunks(0):
                f()
            n = len(items)
            for step in range(n + 2):
                if step < n:
                    h, m = items[step]
                    if m == 0 and h + 1 < 16:
                        pending = {"fs": proj_chunks(h + 1), "i": 0}
                    phaseA(step, h, m)
                    if pending and pending["i"] < len(pending["fs"]):
                        per = -(-len(pending["fs"]) // max(NM, 1))
                        for _ in range(per):
                            if pending["i"] < len(pending["fs"]):
                                pending["fs"][pending["i"]]()
                                pending["i"] += 1
                if 0 <= step - 1 < n:
                    phaseB(step - 1, *items[step - 1])
                if 0 <= step - 2 < n:
                    phaseC(step - 2, *items[step - 2])
            k.barrier()
        proj_resid(P, OT, DC, wout, xres, ybuf)


def ro(ap):
    b = Buf(ap)
    return b


WSHAPES = {"gla_w_in": (D, 6176), "gla_w_out": (D, D), "na_w_in": (D, 3 * D), "na_w_out": (D, D),
           "ffn_w_up": (D, 2 * FF), "ffn_w_down": (FF, D)}
SUB_W = {"gla": ("gla_w_in", "gla_w_out"), "na": ("na_w_in", "na_w_out"), "ffn": ("ffn_w_up", "ffn_w_down")}


def declare_weights(P, subs):
    P.wfull, P.wsh, P.wbounce = {}, {}, {}
    for kind, idx, li in subs:
        for base in SUB_W[kind]:
            name = "%s_%d" % (base, idx)
            Kr, N = WSHAPES[base]
            P.wsh[name] = P.din(name, [Kr // 8, N])
            P.wbounce[name] = P.dscr(name + "_b", [Kr // 8, N], BF16)
            P.wfull[name] = P.dscr(name + "_f", [Kr, N], BF16)


def gather_weights(P, kind, idx):
    k = P.k
    for base in SUB_W[kind]:
        name = "%s_%d" % (base, idx)
        sh, bo, fu = P.wsh[name], P.wbounce[name], P.wfull[name]
        Kr, N = WSHAPES[base]
        rows = Kr // 8
        v = "(a r) n -> a r n"
        nsp = 4
        for a in range(nsp):
            r0_, r1_ = a * rows // nsp, (a + 1) * rows // nsp
            k.dma(k.pool, bo.t[r0_:r1_, :], sh.t[r0_:r1_, :], reads=[sh], writes=[bo])
        k.acc(k.pool, [bo], [fu])
        ins = k.nc.gpsimd.collective_compute("AllGather", ALU.bypass, replica_groups=[list(range(8))],
                                              ins=[bo.th.ap().opt()], outs=[fu.th.ap().opt()])
        P.xch["cnt"] += 1
        ins.then_inc(P.xch["sem"])
        k.done((P.xch["sem"], P.xch["cnt"]), [bo], [fu])


def make_consts(P, st):
    k = P.k
    c = {}
    ones = k.sb(st, [128, 128], F32, "ones")
    k.op(k.pool, lambda e: e.memset(ones[:, :], 1.0), writes=[ones])
    eps = k.sb(st, [128, 2], F32, "eps")
    k.op(k.pool, lambda e: e.memset(eps[:, 0:1], LN_EPS), writes=[eps])
    k.op(k.pool, lambda e: e.memset(eps[:, 1:2], RMS_EPS), writes=[eps])
    c["ones"] = ones
    c["eps_ln"] = eps
    flags = k.sb(st, [128, 2], F32, "flags")
    k.dma(k.sp, flags[:, :], P.dram_in["flags"].t, writes=[flags])
    c["flags"] = flags
    gb = k.sb(st, [128, 2 * DEPTH * 2 * DC], F32, "gb")
    k.dma(k.sp, gb[:, :], P.dram_in["gb"].t, writes=[gb])
    c["gb"] = gb
    cv = k.sb(st, [128, DEPTH * 88 * 4], F32, "cv")
    k.dma(k.sp, cv[:, :], P.dram_in["cv"].t, writes=[cv])
    c["cv"] = cv
    idf = k.sb(st, [128, 128], F32, "idf")
    k.dma(k.sp, idf[:, :], P.dram_in["ident"].t, writes=[idf])
    idb = k.sb(st, [128, 128], BF16, "idb")
    k.op(k.dve, lambda e: e.tensor_copy(out=idb[:, :], in_=idf[:, :]), reads=[idf], writes=[idb])
    c["ident_bf"] = idb
    c["ident_f"] = idf
    return c


def build(cfg):
    P = Prog(cfg)
    k = P.k
    subs = cfg["subs"]
    kinds = set(s[0] for s in subs)
    x_in = P.din("xT0", [D, TL])
    P.din("flags", [128, 2])
    P.din("gb", [128, 2 * DEPTH * 2 * DC])
    P.din("cv", [128, DEPTH * 88 * 4])
    P.din("ident", [128, 128])
    declare_weights(P, subs)
    P.gla_params = {}
    P.na_params = {}
    for kind, idx, li in subs:
        if kind == "gla":
            P.gla_params[idx] = {"wg": P.din("gla_wg_%d" % idx, [16, 2, 1024]),
                                 "bg": P.din("gla_bg_%d" % idx, [128, 16]),
                                 "ng": P.din("gla_ng_%d" % idx, [128, 512])}
        if kind == "na":
            P.na_params[idx] = {"tb": P.din("na_tb_%d" % idx, [16, 8, 128, 640])}
    if "gla" in kinds:
        P.din("gla_masks", [128, 2, 128])
        P.gla_o1 = P.dscr("gla_o1", [TL, D])
        P.gla_snd = P.dscr("gla_snd", [1024, 512])
        P.gla_rcv = P.dscr("gla_rcv", [2048, 512])
    out = P.dout("outT", [D, TL])
    ybuf = P.dscr("ybuf", [D, TL])
    xresA = P.dscr("xresA", [D, TL])
    xresB = P.dscr("xresB", [D, TL])
    xch = {"snd": P.dscr("xsnd", [D, 256], BF16), "rcv": P.dscr("xrcv", [2 * D, 256], BF16),
           "sem": k.new_sem("cc"), "cnt": 0}
    P.xch = xch
    with ExitStack() as st:
        consts = make_consts(P, st)
        xT = k.sb(st, [128, DC, XW], BF16, "xT")
        halo_of = {"ffn": "ffn", "na": "na", "gla": None}
        gather_weights(P, subs[0][0], subs[0][1])
        with ExitStack() as st1:
            load_xT(P, st1, x_in, xT)
            if halo_of[subs[0][0]] is not None:
                exchange_halo(P, st1, xT, halo_of[subs[0][0]], xch, consts)
            k.barrier()
        xres = x_in
        for si, (kind, idx, li) in enumerate(subs):
            last = si == len(subs) - 1
            xres_out = out if last else (xresA if si % 2 == 0 else xresB)
            nxt = None if last else halo_of[subs[si + 1][0]]
            if not last:
                gather_weights(P, subs[si + 1][0], subs[si + 1][1])
            if kind == "ffn":
                ffn_stage(P, xT, P.wfull["ffn_w_up_%d" % idx], P.wfull["ffn_w_down_%d" % idx], consts["cv"], idx, xres, ybuf)
            elif kind == "gla":
                gla_stage(P, xT, idx, xres, ybuf, consts)
            elif kind == "na":
                na_stage(P, xT, idx, xres, ybuf, consts)
            ln_stage(P, ybuf, consts["gb"], li, xres_out, xT, consts, nxt, xch)
            xres = xres_out
        k.barrier()
    k.es.close()
    return P


def na_bias_table(rpb, flip):
    NEG = np.float32(-1e9)
    H = rpb.shape[0]
    tab = np.full((H, 64, 16, 640), NEG, np.float32)
    rows_l = np.arange(16)
    for r in range(16):
        p0 = min(max(r // 2 - 2, 0), 5)
        for i in range(10):
            e = 2 * p0 + i
            if e < 16:
                kr_loc, same = e, True
            else:
                kr_loc, same = 15 - (e - 16), False
            for c in range(64):
                if not flip:
                    gq_r, gq_c = r, c
                    if same:
                        gk_r = kr_loc
                        gk_c = np.arange(64)
                    else:
                        gk_r = 31 - kr_loc
                        gk_c = 63 - np.arange(64)
                else:
                    gq_r, gq_c = 31 - r, 63 - c
                    if same:
                        gk_r = 31 - kr_loc
                        gk_c = 63 - np.arange(64)
                    else:
                        gk_r = kr_loc
                        gk_c = np.arange(64)
                rs = min(max(gq_r - 4, 0), 24)
                if not (rs <= gk_r < rs + 8):
                    continue
                wc = min(max(gq_c - 8, 0), 48)
                valid = (gk_c >= wc) & (gk_c < wc + 16)
                dr = gk_r - gq_r + 7
                dc = np.clip(gk_c - gq_c, -15, 15) + 15
                vals = rpb[:, dr, :][:, dc]
                blk = tab[:, c, r, i * 64:(i + 1) * 64]
                blk[:, valid] = vals[:, valid]
    return tab


def host_inputs(inp, subs):
    x = np.asarray(inp["x"], np.float32)
    maps = []
    gbcols = []
    for layer in range(DEPTH):
        for g_, b_ in ((inp["ln_mix_g"], inp["ln_mix_b"]), (inp["ln_ffn_g"], inp["ln_ffn_b"])):
            gbcols.append(np.asarray(g_[layer], np.float32).reshape(DC, 128).T)
            gbcols.append(np.asarray(b_[layer], np.float32).reshape(DC, 128).T)
    gb = np.ascontiguousarray(np.concatenate(gbcols, axis=1))
    cw = np.asarray(inp["ffn_conv_w"], np.float32)
    cb = np.asarray(inp["ffn_conv_b"], np.float32)

    def conv_tab(flip):
        w = cw[:, ::-1, :] if flip else cw
        t = np.concatenate([w, cb[:, None, :]], axis=1)
        t = t.reshape(DEPTH, 4, 88, 128).transpose(3, 0, 2, 1)
        return np.ascontiguousarray(t.reshape(128, DEPTH * 88 * 4))

    cvs = [conv_tab(False), conv_tab(True)]
    ident = np.eye(128, dtype=np.float32)
    tri_le = np.triu(np.ones((128, 128), np.float32))
    tri_lt = np.triu(np.ones((128, 128), np.float32), 1)
    gmask = [np.ascontiguousarray(np.stack([tri_le, tri_lt.T], axis=1)),
             np.ascontiguousarray(np.stack([tri_lt, tri_le.T], axis=1))]
    per_half = [{}, {}]
    for kind, idx, li in subs:
        if kind == "gla":
            wgu = np.asarray(inp["gla_w_gate_up"][idx], np.float32)
            bgt = np.asarray(inp["gla_b_gate"][idx], np.float32)
            ngt = np.asarray(inp["gla_norm_g"][idx], np.float32)
            for h in range(2):
                zs = [0, 1] if h == 0 else [1, 0]
                per_half[h]["gla_wg_%d" % idx] = np.ascontiguousarray(wgu[zs].transpose(1, 0, 2))
                per_half[h]["gla_bg_%d" % idx] = np.ascontiguousarray(
                    bgt[zs].reshape(2, 8, 128).transpose(2, 0, 1).reshape(128, 16))
                per_half[h]["gla_ng_%d" % idx] = np.ascontiguousarray(np.tile(ngt[None, :], (128, 1)))
        if kind == "na":
            rpb = np.asarray(inp["na_rpb"][idx], np.float32)
            for h in range(2):
                tab = na_bias_table(rpb, h == 1)
                per_half[h]["na_tb_%d" % idx] = np.ascontiguousarray(tab.transpose(0, 2, 1, 3).reshape(16, 8, 128, 640))
    for c in range(8):
        b, h = c // 2, c % 2
        xs = x[b, h * TL:(h + 1) * TL, :]
        if h == 1:
            xs = xs[::-1]
        m = {"xT0": np.ascontiguousarray(xs.T),
             "flags": np.ascontiguousarray(np.tile(np.array([[1.0 - h, float(h)]], np.float32), (128, 1))),
             "gb": gb, "cv": cvs[h], "ident": ident}
        if any(s_[0] == "gla" for s_ in subs):
            m["gla_masks"] = gmask[h]
        m.update(per_half[h])
        for kind, idx, li in subs:
            for base in SUB_W[kind]:
                Kr, N = WSHAPES[base]
                rows = Kr // 8
                m["%s_%d" % (base, idx)] = np.ascontiguousarray(np.asarray(inp[base][idx], np.float32)[c * rows:(c + 1) * rows])
        maps.append(m)
    return maps


def host_output(res):
    out = np.empty((4, 2 * TL, D), np.float32)
    for c in range(8):
        b, h = c // 2, c % 2
        o = res[c]["outT"].T
        if h == 1:
            o = o[::-1]
        out[b, h * TL:(h + 1) * TL, :] = o
    return out


FULL_SUBS = [("gla", 0, 0), ("ffn", 0, 1), ("na", 0, 2), ("ffn", 1, 3),
             ("gla", 1, 4), ("ffn", 2, 5), ("na", 1, 6), ("ffn", 3, 7)]


def run(inp, subs, **cfg):
    P = build(dict(cfg, subs=subs))
    maps = host_inputs(inp, subs)
    res = run_bass_kernel_spmd(P.nc, maps, core_ids=list(range(8)))
    return host_output(res.results)


def kernel(**inputs):
    return run(inputs, FULL_SUBS)
```

```python
import numpy as np
from contextlib import ExitStack
import concourse.bass as bass
import concourse.mybir as mybir
from concourse.bass_utils import run_bass_kernel_spmd

F32 = mybir.dt.float32
BF16 = mybir.dt.bfloat16
AF = mybir.ActivationFunctionType
ALU = mybir.AluOpType
AX = mybir.AxisListType

D = 2048
DC = 16
TL = 1024
XW = 1280
DEPTH = 4
FF = 5632
FC = 44
ALPHA = (2.0 * DEPTH) ** 0.25
LN_EPS = 1e-5
RMS_EPS = 1e-6
PAIRS = [[0, 1], [2, 3], [4, 5], [6, 7]]


class Eng:
    def __init__(self, K, name, e):
        self.K, self.name, self.e = K, name, e
        self.sem = K.new_sem("e_" + name)
        self.n = 0
        self.seen = {}

    def wait(self, tok):
        if tok is None:
            return
        sem, v = tok
        k = id(sem)
        if self.seen.get(k, 0) >= v:
            return
        self.e.wait_ge(sem, v)
        self.seen[k] = v

    def sig(self, ins):
        self.n += 1
        ins.then_inc(self.sem, 1)
        return (self.sem, self.n)


class Buf:
    def __init__(self, t=None):
        self.t = t
        self.w = None
        self.r = {}

    def __getitem__(self, idx):
        return self.t[idx]


class K:
    def __init__(self, nc):
        self.nc = nc
        self.es = ExitStack()
        self.nsem = 0
        self.pe = Eng(self, "pe", nc.tensor)
        self.dve = Eng(self, "dve", nc.vector)
        self.act = Eng(self, "act", nc.scalar)
        self.pool = Eng(self, "pool", nc.gpsimd)
        self.sp = Eng(self, "sp", nc.sync)
        self.engs = [self.pe, self.dve, self.act, self.pool, self.sp]
        self.dq = {}
        for q in (self.sp, self.pool, self.act):
            sems = [self.new_sem("d_%s%d" % (q.name, i)) for i in range(8)]
            self.dq[q.name] = {"sems": sems, "cnt": [0] * 8, "i": 0}
        self.uid = 0

    def new_sem(self, name):
        self.nsem += 1
        return self.es.enter_context(self.nc.semaphore(name))

    def name(self, p):
        self.uid += 1
        return "%s_%d" % (p, self.uid)

    def acc(self, eng, reads, writes):
        for b in reads:
            eng.wait(b.w)
        for b in writes:
            eng.wait(b.w)
            for t in list(b.r.values()):
                eng.wait(t)

    def done(self, tok, reads, writes):
        for b in reads:
            b.r[id(tok[0])] = tok
        for b in writes:
            b.w = tok
            b.r = {}

    def op(self, eng, fn, reads=(), writes=()):
        self.acc(eng, reads, writes)
        ins = fn(eng.e)
        tok = eng.sig(ins)
        self.done(tok, reads, writes)
        return tok

    def mm_group(self, fns, reads=(), writes=()):
        eng = self.pe
        self.acc(eng, reads, writes)
        ins = None
        for fn in fns:
            ins = fn(eng.e)
        tok = eng.sig(ins)
        self.done(tok, reads, writes)
        return tok

    def dma(self, q, out, in_, reads=(), writes=(), **kw):
        st = self.dq[q.name]
        i = st["i"]
        st["i"] = (i + 1) % len(st["sems"])
        sem = st["sems"][i]
        if st["cnt"][i] > 0:
            q.wait((sem, 16 * st["cnt"][i]))
        self.acc(q, reads, writes)
        ins = q.e.dma_start(out=out, in_=in_, **kw)
        st["cnt"][i] += 1
        ins.then_inc(sem, 16)
        tok = (sem, 16 * st["cnt"][i])
        self.done(tok, reads, writes)
        return tok

    def all_tokens(self):
        toks = [(e.sem, e.n) for e in self.engs if e.n > 0]
        for st in self.dq.values():
            for s, c in zip(st["sems"], st["cnt"]):
                if c > 0:
                    toks.append((s, 16 * c))
        return toks

    def barrier(self):
        toks = self.all_tokens()
        for e in self.engs:
            for t in toks:
                e.wait(t)

    def sb(self, st, shape, dt, name="t"):
        t = st.enter_context(self.nc.sbuf_tensor(self.name(name), list(shape), dt))
        return Buf(t)

    def ps(self, st, shape, dt=F32, name="ps"):
        t = st.enter_context(self.nc.psum_tensor(self.name(name), list(shape), dt))
        return Buf(t)


class Prog:
    def __init__(self, cfg):
        self.cfg = cfg
        nc = bass.Bass("TRN2", target_bir_lowering=False)
        self.nc = nc
        self.k = K(nc)
        self.dram_in = {}
        self.dram_bufs = {}

    def din(self, name, shape, dt=F32):
        t = self.nc.dram_tensor(name, list(shape), dt, kind="ExternalInput")
        b = Buf(t.ap())
        b.th = t
        self.dram_in[name] = b
        return b

    def dout(self, name, shape, dt=F32):
        t = self.nc.dram_tensor(name, list(shape), dt, kind="ExternalOutput")
        b = Buf(t.ap())
        b.th = t
        return b

    def dscr(self, name, shape, dt=F32):
        t = self.nc.dram_tensor(name, list(shape), dt)
        b = Buf(t.ap())
        b.th = t
        return b


def load_xT(P, st, xsrc, xT):
    k = P.k
    tmp = [k.sb(st, [128, 1024], F32, "ld") for _ in range(2)]
    for c in range(DC):
        t = tmp[c % 2]
        k.dma(k.sp, t[:, :], xsrc[c * 128:(c + 1) * 128, :], reads=[xsrc], writes=[t])
        eng = k.dve if c % 2 == 0 else k.pool
        k.op(eng, lambda e, t=t, c=c: e.tensor_copy(out=xT[:, c, 0:TL], in_=t[:, :]),
             reads=[t], writes=[xT])


def proj_resid(P, actT, KC, W, xres, ybuf, wname="w"):
    k = P.k
    with ExitStack() as st:
        KB = 11 if KC == 44 else 16
        NKB = KC // KB
        NW = 3
        wts = [k.sb(st, [128, KB, 256], BF16, "wres") for _ in range(NW)]
        pss = [k.ps(st, [128, 512], F32, "psr") for _ in range(8)]
        xr = [k.sb(st, [128, 512], F32, "xr") for _ in range(8)]
        yt = [k.sb(st, [128, 512], F32, "yt") for _ in range(4)]
        Wv = W.t.rearrange("(kc p) d -> p kc d", p=128)
        loads = [(p, kb) for p in range(8) for kb in range(NKB)]

        def issue(i):
            p, kb = loads[i]
            wt = wts[i % NW]
            k.dma(k.pool, wt[:, :, :], Wv[:, kb * KB:(kb + 1) * KB, p * 256:(p + 1) * 256],
                  reads=[W], writes=[wt])

        for i in range(min(NW - 1, len(loads))):
            issue(i)
        li = 0
        ei = 0
        for p in range(8):
            banks = pss[(p % 2) * 4:(p % 2) * 4 + 4]
            for dd in range(2):
                for th in range(2):
                    dch = p * 2 + dd
                    x_ = xr[(p % 2) * 4 + dd * 2 + th]
                    k.dma(k.sp, x_[:, :], xres[dch * 128:(dch + 1) * 128, th * 512:(th + 1) * 512],
                          reads=[xres], writes=[x_])
            for kb in range(NKB):
                if li + NW - 1 < len(loads):
                    issue(li + NW - 1)
                wt = wts[li % NW]
                li += 1
                fns = []
                for dd in range(2):
                    for th in range(2):
                        ps = banks[dd * 2 + th]
                        for kk in range(KB):
                            kc = kb * KB + kk
                            fns.append(lambda e, ps=ps, wt=wt, kk=kk, kc=kc, dd=dd, th=th: e.matmul(
                                ps[:, :], wt[:, kk, dd * 128:(dd + 1) * 128],
                                actT[:, kc, th * 512:(th + 1) * 512],
                                start=(kc == 0), stop=(kc == KC - 1)))
                k.mm_group(fns, reads=[wt, actT], writes=banks)
            for dd in range(2):
                for th in range(2):
                    ps = banks[dd * 2 + th]
                    dch = p * 2 + dd
                    x_ = xr[(p % 2) * 4 + dd * 2 + th]
                    y_ = yt[ei % 4]
                    ei += 1
                    k.op(k.dve, lambda e, x_=x_, y_=y_, ps=ps: e.scalar_tensor_tensor(
                        out=y_[:, :], in0=x_[:, :], scalar=ALPHA, in1=ps[:, :],
                        op0=ALU.mult, op1=ALU.add), reads=[x_, ps], writes=[y_])
                    k.dma(k.sp, ybuf[dch * 128:(dch + 1) * 128, th * 512:(th + 1) * 512], y_[:, :],
                          reads=[y_], writes=[ybuf])
        k.barrier()


def ln_stage(P, ybuf, gb, li, xres_out, xT, consts, halo, xch):
    k = P.k
    with ExitStack() as st:
        ones = consts["ones"]
        ytl2 = [[k.sb(st, [128, 512], F32, "ytl") for _ in range(DC)] for _ in range(2)]
        sq = [k.sb(st, [128, 512], F32, "sq") for _ in range(2)]
        ps_s = k.ps(st, [128, 512], F32, "ps_s")
        ps_q = k.ps(st, [128, 512], F32, "ps_q")
        mean = k.sb(st, [128, 512], F32, "mean")
        rstd = k.sb(st, [128, 512], F32, "rstd")
        tmp = k.sb(st, [128, 512], F32, "tmp")
        t1 = [k.sb(st, [128, 512], F32, "t1") for _ in range(2)]
        xo = [k.sb(st, [128, 512], F32, "xo") for _ in range(3)]
        for th in range(2):
            for c in range(DC):
                k.dma(k.sp, ytl2[th][c][:, :], ybuf[c * 128:(c + 1) * 128, th * 512:(th + 1) * 512],
                      reads=[ybuf], writes=[ytl2[th][c]])
        for th in range(2):
            ts = slice(th * 512, (th + 1) * 512)
            ytl = ytl2[th]
            for c in range(DC):
                s_ = sq[c % 2]
                k.op(k.act, lambda e, s_=s_, c=c, ytl=ytl: e.activation(out=s_[:, :], in_=ytl[c][:, :], func=AF.Square),
                     reads=[ytl[c]], writes=[s_])
                k.mm_group([lambda e, c=c, ytl=ytl: e.matmul(ps_s[:, :], ones[:, :], ytl[c][:, :],
                                                    start=(c == 0), stop=(c == DC - 1))],
                           reads=[ones, ytl[c]], writes=[ps_s])
                k.mm_group([lambda e, c=c, s_=s_: e.matmul(ps_q[:, :], ones[:, :], s_[:, :],
                                                           start=(c == 0), stop=(c == DC - 1))],
                           reads=[ones, s_], writes=[ps_q])
            k.op(k.act, lambda e: e.activation(out=mean[:, :], in_=ps_s[:, :], func=AF.Copy, scale=1.0 / D),
                 reads=[ps_s], writes=[mean])
            k.op(k.act, lambda e: e.activation(out=tmp[:, :], in_=ps_q[:, :], func=AF.Copy, scale=1.0 / D),
                 reads=[ps_q], writes=[tmp])
            k.op(k.dve, lambda e: e.tensor_tensor(out=rstd[:, :], in0=mean[:, :], in1=mean[:, :], op=ALU.mult),
                 reads=[mean], writes=[rstd])
            k.op(k.dve, lambda e: e.tensor_tensor(out=tmp[:, :], in0=tmp[:, :], in1=rstd[:, :], op=ALU.subtract),
                 reads=[tmp, rstd], writes=[tmp])
            k.op(k.act, lambda e: e.activation(out=tmp[:, :], in_=tmp[:, :], func=AF.Sqrt, bias=consts["eps_ln"][:, 0:1]),
                 reads=[tmp], writes=[tmp])
            k.op(k.dve, lambda e: e.reciprocal(out=rstd[:, :], in_=tmp[:, :]), reads=[tmp], writes=[rstd])
            for c in range(DC):
                a = t1[c % 2]
                o = xo[c % 3]
                k.op(k.dve, lambda e, a=a, c=c, ytl=ytl: e.tensor_tensor(out=a[:, :], in0=ytl[c][:, :], in1=mean[:, :],
                                                                op=ALU.subtract), reads=[ytl[c], mean], writes=[a])
                k.op(k.dve, lambda e, a=a: e.tensor_tensor(out=a[:, :], in0=a[:, :], in1=rstd[:, :], op=ALU.mult),
                     reads=[a, rstd], writes=[a])
                gcol = (li * 2 + 0) * DC + c
                bcol = (li * 2 + 1) * DC + c
                k.op(k.act, lambda e, a=a, o=o, gcol=gcol, bcol=bcol: e.activation(
                    out=o[:, :], in_=a[:, :], func=AF.Identity, scale=gb[:, gcol:gcol + 1], bias=gb[:, bcol:bcol + 1]),
                    reads=[a, gb], writes=[o])
                k.op(k.pool, lambda e, o=o, c=c, ts=ts: e.tensor_copy(out=xT[:, c, ts], in_=o[:, :]),
                     reads=[o], writes=[xT])
                k.dma(k.sp, xres_out[c * 128:(c + 1) * 128, ts], o[:, :], reads=[o], writes=[xres_out])
        if halo is not None:
            exchange_halo(P, st, xT, halo, xch, consts)
        k.barrier()


def exchange_halo(P, st, xT, halo, xch, consts):
    k = P.k
    H = 64 if halo == "ffn" else 256
    snd, rcv = xch["snd"], xch["rcv"]
    flags = consts["flags"]
    for c in range(DC):
        k.dma(k.sp, snd[c * 128:(c + 1) * 128, 0:H], xT[:, c, TL - H:TL], reads=[xT], writes=[snd])
    k.acc(k.pool, [snd], [rcv])
    ins = k.nc.gpsimd.collective_compute("AllGather", ALU.bypass, replica_groups=PAIRS,
                                          ins=[snd.th.ap().opt()], outs=[rcv.th.ap().opt()])
    sem = xch["sem"]
    xch["cnt"] += 1
    ins.then_inc(sem)
    tok = (sem, xch["cnt"])
    k.done(tok, [snd], [rcv])
    r0 = k.sb(st, [128, DC, H], BF16, "r0")
    r1 = k.sb(st, [128, DC, H], BF16, "r1")
    rv = rcv.t.rearrange("(r c p) h -> r p c h", r=2, p=128)
    k.dma(k.sp, r0[:, :, :], rv[0, :, :, 0:H], reads=[rcv], writes=[r0])
    k.dma(k.sp, r1[:, :, :], rv[1, :, :, 0:H], reads=[rcv], writes=[r1])
    k.op(k.dve, lambda e: e.tensor_scalar(out=r0[:, :, :], in0=r0[:, :, :], scalar1=flags[:, 1:2], scalar2=None,
                                          op0=ALU.mult), reads=[r0, flags], writes=[r0])
    if halo == "ffn":
        k.op(k.dve, lambda e: e.scalar_tensor_tensor(out=xT[:, :, TL:TL + 1], in0=r1[:, :, 63:64], scalar=flags[:, 0:1],
                                                     in1=r0[:, :, 63:64], op0=ALU.mult, op1=ALU.add),
             reads=[r0, r1, flags], writes=[xT])
    else:
        for j in range(4):
            src = slice((3 - j) * 64, (4 - j) * 64)
            k.op(k.dve, lambda e, j=j, src=src: e.scalar_tensor_tensor(
                out=xT[:, :, TL + 64 * j:TL + 64 * (j + 1)], in0=r1[:, :, src], scalar=flags[:, 0:1],
                in1=r0[:, :, src], op0=ALU.mult, op1=ALU.add), reads=[r0, r1, flags], writes=[xT])


def ffn_stage(P, xT, wup, wdn, cv, layer, xres, ybuf):
    k = P.k
    NT = TL + 1
    splits = [(0, 342), (342, 342), (684, 341)]
    with ExitStack() as st:
        gT = k.sb(st, [128, FC, TL], BF16, "gT")
        with ExitStack() as st2:
            NW = 3
            wts = [k.sb(st2, [128, DC, 256], BF16, "wup") for _ in range(NW)]
            banks = [k.ps(st2, [128, 512], F32, "psu") for _ in range(6)]
            hs = [k.sb(st2, [128, NT], F32, "hs") for _ in range(2)]
            cab = [k.sb(st2, [128, TL], F32, "cab") for _ in range(4)]
            ga = [k.sb(st2, [128, TL], F32, "ga") for _ in range(2)]
            Wv = wup.t.rearrange("(kc p) f -> p kc f", p=128)
            NG = FC // 2
            loads = [(g, h) for g in range(NG) for h in range(2)]

            def issue(i):
                g, h = loads[i]
                col = h * FF + g * 256
                wt = wts[i % NW]
                k.dma(k.pool, wt[:, :, :], Wv[:, :, col:col + 256], reads=[wup], writes=[wt])

            for i in range(NW - 1):
                issue(i)
            ui = 0
            for i, (g, h) in enumerate(loads):
                if i + NW - 1 < len(loads):
                    issue(i + NW - 1)
                wt = wts[i % NW]
                for jj in range(2):
                    j = g * 2 + jj
                    ch = h * FC + j
                    bset = banks[(ui % 2) * 3:(ui % 2) * 3 + 3]
                    fns = []
                    for si, (s0, sn) in enumerate(splits):
                        for kc in range(DC):
                            fns.append(lambda e, ps=bset[si], kc=kc, s0=s0, sn=sn, jj=jj, wt=wt: e.matmul(
                                ps[:, 0:sn], wt[:, kc, jj * 128:(jj + 1) * 128], xT[:, kc, s0:s0 + sn],
                                start=(kc == 0), stop=(kc == DC - 1)))
                    k.mm_group(fns, reads=[wt, xT], writes=bset)
                    h_ = hs[ui % 2]
                    for si, (s0, sn) in enumerate(splits):
                        k.op(k.act, lambda e, h_=h_, ps=bset[si], s0=s0, sn=sn: e.activation(
                            out=h_[:, s0:s0 + sn], in_=ps[:, 0:sn], func=AF.Copy),
                            reads=[bset[si]], writes=[h_])
                    c_ = cab[(h * 2 + jj)]
                    pc = (layer * 88 + ch) * 4
                    k.op(k.act, lambda e, c_=c_, h_=h_, pc=pc: e.activation(
                        out=c_[:, :], in_=h_[:, 0:TL], func=AF.Identity, scale=cv[:, pc + 1:pc + 2],
                        bias=cv[:, pc + 3:pc + 4]), reads=[h_, cv], writes=[c_])
                    k.op(k.dve, lambda e, c_=c_, h_=h_, pc=pc: e.scalar_tensor_tensor(
                        out=c_[:, 1:TL], in0=h_[:, 0:TL - 1], scalar=cv[:, pc:pc + 1], in1=c_[:, 1:TL],
                        op0=ALU.mult, op1=ALU.add), reads=[h_, cv, c_], writes=[c_])
                    k.op(k.dve, lambda e, c_=c_, h_=h_, pc=pc: e.scalar_tensor_tensor(
                        out=c_[:, 0:TL], in0=h_[:, 1:TL + 1], scalar=cv[:, pc + 2:pc + 3], in1=c_[:, 0:TL],
                        op0=ALU.mult, op1=ALU.add), reads=[h_, cv, c_], writes=[c_])
                    ui += 1
                if h == 1:
                    for jj in range(2):
                        j = g * 2 + jj
                        g_ = ga[jj]
                        ca, cb = cab[jj], cab[2 + jj]
                        k.op(k.act, lambda e, g_=g_, ca=ca: e.activation(out=g_[:, :], in_=ca[:, :], func=AF.Gelu),
                             reads=[ca], writes=[g_])
                        k.op(k.dve, lambda e, g_=g_, cb=cb, j=j: e.tensor_tensor(
                            out=gT[:, j, :], in0=g_[:, :], in1=cb[:, :], op=ALU.mult),
                            reads=[g_, cb], writes=[gT])
            k.barrier()
        proj_resid(P, gT, FC, wdn, xres, ybuf)


class WStream:
    def __init__(self, P, st, W, n=3):
        self.k = P.k
        self.W = W
        self.Wv = W.t.rearrange("(kc p) f -> p kc f", p=128)
        self.tiles = [P.k.sb(st, [128, DC, 256], BF16, "wst") for _ in range(n)]
        self.i = 0

    def load(self, col, width=256):
        t = self.tiles[self.i % len(self.tiles)]
        self.i += 1
        self.k.dma(self.k.pool, t[:, :, 0:width], self.Wv[:, :, col:col + width], reads=[self.W], writes=[t])
        return t


def gla_stage(P, xT, j, xres, ybuf, consts):
    k = P.k
    win, wout = P.wfull["gla_w_in_%d" % j], P.wfull["gla_w_out_%d" % j]
    gp = P.gla_params[j]
    o1 = P.gla_o1
    ssnd, srcv = P.gla_snd, P.gla_rcv
    flags = consts["flags"]
    with ExitStack() as st:
        ogT = k.sb(st, [128, DC, TL], BF16, "ogT")
        with ExitStack() as s2:
            ws = WStream(P, s2, win, 2)
            wg = k.sb(s2, [16, 2, 1024], F32, "wg")
            k.dma(k.sp, wg[:, :, :], gp["wg"].t, writes=[wg])
            bg = k.sb(s2, [128, 16], F32, "bg")
            k.dma(k.sp, bg[:, :], gp["bg"].t, writes=[bg])
            nbg = k.sb(s2, [128, 16], F32, "nbg")
            k.op(k.dve, lambda e: e.tensor_scalar(out=nbg[:, :], in0=bg[:, :], scalar1=-1.0, scalar2=None, op0=ALU.mult),
                 reads=[bg], writes=[nbg])
            ng = k.sb(s2, [128, 512], F32, "ng")
            k.dma(k.sp, ng[:, :], gp["ng"].t, writes=[ng])
            masks = k.sb(s2, [128, 2, 128], F32, "masks")
            k.dma(k.sp, masks[:, :, :], P.dram_in["gla_masks"].t, writes=[masks])
            ident = consts["ident_bf"]
            onesf = consts["ones"]
            lrp = [k.sb(s2, [16, TL], F32, "lrp") for _ in range(2)]
            psA = [k.ps(s2, [128, 512], F32, "psA") for _ in range(2)]
            ps_lr = psA
            s3 = ExitStack()
            lr = [k.sb(s3, [16, TL], F32, "lr") for _ in range(2)]
            wt = ws.load(6144, 32)
            for z in range(2):
                for th in range(2):
                    ps = ps_lr[th]
                    k.mm_group([lambda e, ps=ps, kc=kc, z=z, th=th: e.matmul(
                        ps[0:16, :], wt[:, kc, z * 16:(z + 1) * 16], xT[:, kc, th * 512:(th + 1) * 512],
                        start=(kc == 0), stop=(kc == DC - 1)) for kc in range(DC)], reads=[wt, xT], writes=[ps])
                    k.op(k.act, lambda e, ps=ps, z=z, th=th: e.activation(
                        out=lr[z][:, th * 512:(th + 1) * 512], in_=ps[0:16, :], func=AF.Copy),
                        reads=[ps], writes=[lr[z]])
            for p in range(2):
                k.op(k.dve, lambda e, p=p: e.tensor_scalar(out=lrp[p][:, :], in0=lr[1 - p][:, :], scalar1=flags[0:16, 1:2],
                                                         scalar2=None, op0=ALU.mult), reads=[lr[1 - p], flags], writes=[lrp[p]])
                k.op(k.dve, lambda e, p=p: e.scalar_tensor_tensor(out=lrp[p][:, :], in0=lr[p][:, :], scalar=flags[0:16, 0:1],
                                                                in1=lrp[p][:, :], op0=ALU.mult, op1=ALU.add),
                     reads=[lr[p], flags, lrp[p]], writes=[lrp[p]])
            k.barrier()
            s3.close()
            lt = k.sb(s2, [128, 2, TL], F32, "lt")
            cs = k.sb(s2, [128, 2, TL], F32, "cs")
            ub = k.sb(s2, [128, 2, 2, 8], F32, "ub")
            dec = k.sb(s2, [128, 2, 8], F32, "dec")
            E = [k.sb(s2, [128, 512], F32, "E") for _ in range(3)]
            qd = k.sb(s2, [128, 2, TL], BF16, "qd")
            ki = k.sb(s2, [128, 2, TL], BF16, "ki")
            kst = k.sb(s2, [128, 2, TL], BF16, "kst")
            kstm = k.sb(s2, [128, 8, 256], BF16, "kstm")
            vt = k.sb(s2, [128, 8, 512], BF16, "vt")
            sg = k.sb(s2, [128, 8, 512], BF16, "sg")
            S32 = k.sb(s2, [128, 2, 512], F32, "S32")
            Sbf = k.sb(s2, [128, 2, 512], BF16, "Sbf")
            r0 = k.sb(s2, [128, 512], F32, "sr0")
            r1 = k.sb(s2, [128, 512], F32, "sr1")
            msk = [k.sb(s2, [128, 128], BF16, "msk") for _ in range(2)]
            ot = [k.sb(s2, [128, 512], F32, "ot") for _ in range(2)]
            o1t = [k.sb(s2, [128, 512], F32, "o1t") for _ in range(2)]
            og = [k.sb(s2, [128, 512], BF16, "og") for _ in range(2)]
            sm = k.sb(s2, [128, 4], F32, "sm")
            junk = k.sb(s2, [128, 512], F32, "junk")
            psS = k.ps(s2, [128, 512], F32, "psS")
            psO = k.ps(s2, [128, 512], F32, "psO")
            psU = [k.ps(s2, [128, 512], F32, "psU") for _ in range(2)]
            psT = k.ps(s2, [128, 1024], BF16, "psT")
            ai = 0

            for p in range(2):
                if p == 1:
                    k.acc(k.pool, [ssnd], [srcv])
                    ins = k.nc.gpsimd.collective_compute("AllGather", ALU.bypass, replica_groups=PAIRS,
                                                          ins=[ssnd.th.ap().opt()], outs=[srcv.th.ap().opt()])
                    P.xch["cnt"] += 1
                    ins.then_inc(P.xch["sem"])
                    k.done((P.xch["sem"], P.xch["cnt"]), [ssnd], [srcv])
                for h in range(4):
                    for q in range(2):
                        ch = h * 2 + q
                        for th in range(2):
                            ps = psA[ai % 2]
                            ai += 1
                            k.mm_group([lambda e, ps=ps, p=p, ch=ch, th=th: e.matmul(
                                ps[:, :], wg[:, p, ch * 128:(ch + 1) * 128], lrp[p][:, th * 512:(th + 1) * 512],
                                start=True, stop=True)], reads=[wg, lrp[p]], writes=[ps])
                            k.op(k.act, lambda e, ps=ps, q=q, th=th, p=p, ch=ch: e.activation(
                                out=lt[:, q, th * 512:(th + 1) * 512], in_=ps[:, :], func=AF.Exp, scale=-1.0,
                                bias=nbg[:, p * 8 + ch:p * 8 + ch + 1]), reads=[ps, nbg], writes=[lt])
                        k.op(k.act, lambda e, q=q: e.activation(out=lt[:, q, :], in_=lt[:, q, :], func=AF.Ln, bias=1.0),
                             reads=[lt], writes=[lt])
                        for c in range(8):
                            k.op(k.dve, lambda e, q=q, c=c: e.tensor_tensor_scan(
                                out=cs[:, q, c * 128:(c + 1) * 128], data0=onesf[:, 0:128], data1=lt[:, q, c * 128:(c + 1) * 128],
                                initial=0.0, op0=ALU.mult, op1=ALU.add), reads=[onesf, lt], writes=[cs])
                        csl = cs[:, q, :].rearrange("p (c t) -> p c t", t=128)[:, :, 127]
                        k.op(k.dve, lambda e, q=q, csl=csl: e.tensor_scalar(out=ub[:, q, 0, :], in0=csl, scalar1=-1.0 / 16.0,
                                                                          scalar2=None, op0=ALU.mult), reads=[cs], writes=[ub])
                        k.op(k.dve, lambda e, q=q, csl=csl: e.tensor_scalar(out=ub[:, q, 1, :], in0=csl, scalar1=1.0 / 16.0,
                                                                          scalar2=None, op0=ALU.mult), reads=[cs], writes=[ub])
                        k.op(k.act, lambda e, q=q: e.activation(out=dec[:, q, :], in_=ub[:, q, 0, :], func=AF.Exp),
                             reads=[ub], writes=[dec])
                        if p == 1:
                            k.op(k.dve, lambda e, q=q: e.tensor_tensor(out=cs[:, q, :], in0=lt[:, q, :], in1=cs[:, q, :],
                                                                      op=ALU.subtract), reads=[lt, cs], writes=[cs])
                    wq = ws.load(h * 256)
                    wk = ws.load(1024 + h * 256)
                    for q in range(2):
                        for th in range(2):
                            for cc in range(4):
                                c = th * 4 + cc
                                tsl = slice(c * 128, (c + 1) * 128)
                                esl = slice(cc * 128, (cc + 1) * 128)
                                if p == 0:
                                    bq, bk, bs_ = 0.0, 0.0, ub[:, q, 0, c:c + 1]
                                else:
                                    bq, bk, bs_ = ub[:, q, 0, c:c + 1], ub[:, q, 1, c:c + 1], 0.0
                                k.op(k.act, lambda e, q=q, tsl=tsl, esl=esl, bq=bq: e.activation(
                                    out=E[0][:, esl], in_=cs[:, q, tsl], func=AF.Exp, scale=-1.0 / 16.0, bias=bq),
                                    reads=[cs, ub], writes=[E[0]])
                                k.op(k.act, lambda e, q=q, tsl=tsl, esl=esl, bk=bk: e.activation(
                                    out=E[1][:, esl], in_=cs[:, q, tsl], func=AF.Exp, scale=1.0 / 16.0, bias=bk),
                                    reads=[cs, ub], writes=[E[1]])
                                k.op(k.act, lambda e, q=q, tsl=tsl, esl=esl, bs_=bs_: e.activation(
                                    out=E[2][:, esl], in_=cs[:, q, tsl], func=AF.Exp, scale=1.0 / 16.0, bias=bs_),
                                    reads=[cs, ub], writes=[E[2]])
                            tsl = slice(th * 512, (th + 1) * 512)
                            ps = psA[ai % 2]
                            ai += 1
                            k.mm_group([lambda e, ps=ps, kc=kc, q=q, tsl=tsl: e.matmul(
                                ps[:, :], wq[:, kc, q * 128:(q + 1) * 128], xT[:, kc, tsl],
                                start=(kc == 0), stop=(kc == DC - 1)) for kc in range(DC)], reads=[wq, xT], writes=[ps])
                            k.op(k.dve, lambda e, ps=ps, q=q, tsl=tsl: e.scalar_tensor_tensor(
                                out=qd[:, q, tsl], in0=ps[:, :], scalar=1.0 / 16.0, in1=E[0][:, :],
                                op0=ALU.mult, op1=ALU.mult), reads=[ps, E[0]], writes=[qd])
                            ps = psA[ai % 2]
                            ai += 1
                            k.mm_group([lambda e, ps=ps, kc=kc, q=q, tsl=tsl: e.matmul(
                                ps[:, :], wk[:, kc, q * 128:(q + 1) * 128], xT[:, kc, tsl],
                                start=(kc == 0), stop=(kc == DC - 1)) for kc in range(DC)], reads=[wk, xT], writes=[ps])
                            k.op(k.dve, lambda e, ps=ps, q=q, tsl=tsl: e.tensor_tensor(
                                out=ki[:, q, tsl], in0=ps[:, :], in1=E[1][:, :], op=ALU.mult),
                                reads=[ps, E[1]], writes=[ki])
                            k.op(k.dve, lambda e, ps=ps, q=q, tsl=tsl: e.tensor_tensor(
                                out=kst[:, q, tsl], in0=ps[:, :], in1=E[2][:, :], op=ALU.mult),
                                reads=[ps, E[2]], writes=[kst])
                    for half in range(2):
                        fns = []
                        for cc in range(4):
                            c = half * 4 + cc
                            for q in range(2):
                                fns.append(lambda e, c=c, cc=cc, q=q: e.transpose(
                                    psT[:, cc * 256 + q * 128:cc * 256 + (q + 1) * 128], kst[:, q, c * 128:(c + 1) * 128],
                                    ident[:, :]))
                        k.mm_group(fns, reads=[kst, ident], writes=[psT])
                        k.op(k.act, lambda e, half=half: e.activation(
                            out=kstm[:, half * 4:(half + 1) * 4, :].rearrange("p a b -> p (a b)"), in_=psT[:, :], func=AF.Copy),
                            reads=[psT], writes=[kstm])
                    for which in range(2 if p == 1 else 1):
                        for hh in range(2):
                            wv = ws.load(2048 + which * 2048 + h * 512 + hh * 256)
                            for c in range(8):
                                ps = psA[ai % 2]
                                ai += 1
                                k.mm_group([lambda e, ps=ps, kc=kc, c=c: e.matmul(
                                    ps[:, 0:256], xT[:, kc, c * 128:(c + 1) * 128], wv[:, kc, :],
                                    start=(kc == 0), stop=(kc == DC - 1)) for kc in range(DC)], reads=[wv, xT], writes=[ps])
                                if which == 0:
                                    k.op(k.act, lambda e, ps=ps, c=c, hh=hh: e.activation(
                                        out=vt[:, c, hh * 256:(hh + 1) * 256], in_=ps[:, 0:256], func=AF.Copy),
                                        reads=[ps], writes=[vt])
                                else:
                                    k.op(k.act, lambda e, ps=ps, c=c, hh=hh: e.activation(
                                        out=sg[:, c, hh * 256:(hh + 1) * 256], in_=ps[:, 0:256], func=AF.Silu),
                                        reads=[ps], writes=[sg])
                    if p == 0:
                        k.op(k.pool, lambda e: e.memset(S32[:, :, :], 0.0), writes=[S32])
                    else:
                        rv = srcv.t.rearrange("(r h q p) f -> r h p q f", r=2, h=4, q=2)
                        for q in range(2):
                            k.dma(k.sp, r0[:, :], rv[0, h, :, q, :], reads=[srcv], writes=[r0])
                            k.dma(k.sp, r1[:, :], rv[1, h, :, q, :], reads=[srcv], writes=[r1])
                            k.op(k.dve, lambda e: e.tensor_scalar(out=r0[:, :], in0=r0[:, :], scalar1=flags[:, 1:2],
                                                                  scalar2=None, op0=ALU.mult), reads=[r0, flags], writes=[r0])
                            k.op(k.dve, lambda e, q=q: e.scalar_tensor_tensor(out=S32[:, q, :], in0=r1[:, :], scalar=flags[:, 0:1],
                                                                              in1=r0[:, :], op0=ALU.mult, op1=ALU.add),
                                 reads=[r0, r1, flags], writes=[S32])
                        k.op(k.act, lambda e: e.activation(out=Sbf[:, :, :], in_=S32[:, :, :], func=AF.Copy),
                             reads=[S32], writes=[Sbf])
                    order = list(range(8)) if p == 0 else list(range(7, -1, -1))
                    for ci, c in enumerate(order):
                        tsl = slice(c * 128, (c + 1) * 128)
                        k.mm_group([lambda e, q=q, tsl=tsl: e.matmul(psS[:, 0:128], ki[:, q, tsl], qd[:, q, tsl],
                                                                      start=(q == 0), stop=(q == 1)) for q in range(2)],
                                   reads=[ki, qd], writes=[psS])
                        m_ = msk[ci % 2]
                        k.op(k.dve, lambda e, m_=m_, p=p: e.tensor_tensor(out=m_[:, :], in0=psS[:, 0:128], in1=masks[:, p, :],
                                                                          op=ALU.mult), reads=[psS, masks], writes=[m_])
                        has_state = not (p == 0 and ci == 0)
                        fns = [lambda e, m_=m_, c=c: e.matmul(psO[:, :], m_[:, :], vt[:, c, :], start=True, stop=not has_state)]
                        if has_state:
                            for q in range(2):
                                fns.append(lambda e, q=q, tsl=tsl: e.matmul(psO[:, :], qd[:, q, tsl], Sbf[:, q, :],
                                                                            start=False, stop=(q == 1)))
                        k.mm_group(fns, reads=[m_, vt, qd, Sbf], writes=[psO])
                        for q in range(2):
                            k.mm_group([lambda e, q=q, c=c: e.matmul(psU[q][:, :], kstm[:, c, q * 128:(q + 1) * 128], vt[:, c, :],
                                                                      start=True, stop=True)],
                                       reads=[kstm, vt], writes=[psU[q]])
                            k.op(k.dve, lambda e, q=q, c=c: e.scalar_tensor_tensor(
                                out=S32[:, q, :], in0=S32[:, q, :], scalar=dec[:, q, c:c + 1], in1=psU[q][:, :],
                                op0=ALU.mult, op1=ALU.add), reads=[S32, dec, psU[q]], writes=[S32])
                        k.op(k.act, lambda e: e.activation(out=Sbf[:, :, :], in_=S32[:, :, :], func=AF.Copy),
                             reads=[S32], writes=[Sbf])
                        if p == 0:
                            o_ = ot[ci % 2]
                            k.op(k.act, lambda e, o_=o_: e.activation(out=o_[:, :], in_=psO[:, :], func=AF.Copy),
                                 reads=[psO], writes=[o_])
                            k.dma(k.sp, o1.t[c * 128:(c + 1) * 128, h * 512:(h + 1) * 512], o_[:, :], reads=[o_], writes=[o1])
                        else:
                            a_ = o1t[ci % 2]
                            o_ = ot[ci % 2]
                            g_ = og[ci % 2]
                            k.dma(k.sp, a_[:, :], o1.t[c * 128:(c + 1) * 128, h * 512:(h + 1) * 512], reads=[o1], writes=[a_])
                            k.op(k.dve, lambda e, a_=a_, o_=o_: e.tensor_tensor(out=o_[:, :], in0=a_[:, :], in1=psO[:, :], op=ALU.add),
                                 reads=[a_, psO], writes=[o_])
                            k.op(k.act, lambda e, o_=o_: e.activation(out=junk[:, :], in_=o_[:, :], func=AF.Square,
                                                                      accum_out=sm[:, 0:1]), reads=[o_], writes=[junk, sm])
                            k.op(k.dve, lambda e: e.tensor_scalar(out=sm[:, 1:2], in0=sm[:, 0:1], scalar1=1.0 / 512.0,
                                                                  scalar2=RMS_EPS, op0=ALU.mult, op1=ALU.add), reads=[sm], writes=[sm])
                            k.op(k.act, lambda e: e.activation(out=sm[:, 2:3], in_=sm[:, 1:2], func=AF.Sqrt), reads=[sm], writes=[sm])
                            k.op(k.dve, lambda e: e.reciprocal(out=sm[:, 3:4], in_=sm[:, 2:3]), reads=[sm], writes=[sm])
                            k.op(k.dve, lambda e, o_=o_: e.scalar_tensor_tensor(out=o_[:, :], in0=o_[:, :], scalar=sm[:, 3:4], in1=ng[:, :],
                                                                              op0=ALU.mult, op1=ALU.mult), reads=[o_, sm, ng], writes=[o_])
                            k.op(k.dve, lambda e, o_=o_, g_=g_, c=c: e.tensor_tensor(out=g_[:, :], in0=o_[:, :], in1=sg[:, c, :], op=ALU.mult),
                                 reads=[o_, sg], writes=[g_])
                            k.mm_group([lambda e, g_=g_, i=i: e.transpose(psT[:, i * 128:(i + 1) * 128], g_[:, i * 128:(i + 1) * 128], ident[:, :])
                                        for i in range(4)], reads=[g_, ident], writes=[psT])
                            k.op(k.act, lambda e, h=h, tsl=tsl: e.activation(
                                out=ogT[:, h * 4:(h + 1) * 4, tsl], in_=psT[:, 0:512].rearrange("p (a b) -> p a b", a=4), func=AF.Copy),
                                reads=[psT], writes=[ogT])
                    if p == 0:
                        sv = ssnd.t.rearrange("(h q p) f -> h p q f", h=4, q=2)
                        k.dma(k.sp, sv[h], S32[:, :, :], reads=[S32], writes=[ssnd])
            k.barrier()
        proj_resid(P, ogT, DC, wout, xres, ybuf)


def na_stage(P, xT, j, xres, ybuf, consts):
    k = P.k
    win, wout = P.wfull["na_w_in_%d" % j], P.wfull["na_w_out_%d" % j]
    tb = P.na_params[j]["tb"]
    ident = consts["ident_bf"]
    KS = [(0, 512), (512, 512), (1024, 256)]
    NM = P.cfg.get('na_rows', 8)
    with ExitStack() as st:
        OT = k.sb(st, [128, DC, TL], BF16, "OT")
        with ExitStack() as s2:
            Wv = win.t.rearrange("(kc p) f -> p kc f", p=128)
            wts = [k.sb(s2, [128, DC, 128], BF16, "naw") for _ in range(6)]
            qT = [k.sb(s2, [128, TL], BF16, "qT") for _ in range(2)]
            kT = [k.sb(s2, [128, XW], BF16, "kT") for _ in range(2)]
            V = [k.sb(s2, [128, 10, 128], BF16, "V") for _ in range(2)]
            bt = [k.sb(s2, [128, 640], F32, "bt") for _ in range(2)]
            sbt = [k.sb(s2, [128, 640], F32, "sbt") for _ in range(2)]
            Pt = [k.sb(s2, [128, 640], BF16, "Pt") for _ in range(3)]
            PT = [k.sb(s2, [128, 5, 128], BF16, "PT") for _ in range(3)]
            Osb = [k.sb(s2, [128, 128], BF16, "Osb") for _ in range(2)]
            sm = [k.sb(s2, [128, 4], F32, "sm") for _ in range(3)]
            psS = [k.ps(s2, [128, 512], F32, "psS") for _ in range(4)]
            psT = [k.ps(s2, [128, 1024], BF16, "psT") for _ in range(2)]
            psT2 = [k.ps(s2, [128, 1024], BF16, "psT2")] * 2
            psO = [k.ps(s2, [128, 512], F32, "psO")] * 2
            state = {"ai": 0}

            def proj_chunks(h):
                hb = h % 2
                w3 = [wts[(h % 2) * 3 + i] for i in range(3)]
                chunks = []

                def ld():
                    for i in range(3):
                        col = i * 2048 + h * 128
                        k.dma(k.pool, w3[i][:, :, :], Wv[:, :, col:col + 128], reads=[win], writes=[w3[i]])

                def cq(th):
                    def f():
                        if th == 0:
                            ld()
                        ps = psS[state["ai"] % 4]
                        state["ai"] += 1
                        k.mm_group([lambda e, ps=ps, kc=kc: e.matmul(
                            ps[:, :], w3[0][:, kc, :], xT[:, kc, th * 512:(th + 1) * 512],
                            start=(kc == 0), stop=(kc == DC - 1)) for kc in range(DC)], reads=[w3[0], xT], writes=[ps])
                        k.op(k.act, lambda e, ps=ps: e.activation(out=qT[hb][:, th * 512:(th + 1) * 512], in_=ps[:, :],
                                                                   func=AF.Copy, scale=128.0 ** -0.5), reads=[ps], writes=[qT[hb]])
                    return f

                def ck(s0, sn):
                    def f():
                        ps = psS[state["ai"] % 4]
                        state["ai"] += 1
                        k.mm_group([lambda e, ps=ps, kc=kc: e.matmul(
                            ps[:, 0:sn], w3[1][:, kc, :], xT[:, kc, s0:s0 + sn],
                            start=(kc == 0), stop=(kc == DC - 1)) for kc in range(DC)], reads=[w3[1], xT], writes=[ps])
                        k.op(k.act, lambda e, ps=ps: e.activation(out=kT[hb][:, s0:s0 + sn], in_=ps[:, 0:sn], func=AF.Copy),
                             reads=[ps], writes=[kT[hb]])
                    return f

                def cv(g4):
                    def f():
                        n = min(4, 10 - g4)
                        ps = psS[state["ai"] % 4]
                        state["ai"] += 1
                        fns = []
                        for i in range(n):
                            pt = g4 + i
                            for kc in range(DC):
                                fns.append(lambda e, ps=ps, kc=kc, pt=pt, i=i: e.matmul(
                                    ps[:, i * 128:(i + 1) * 128], xT[:, kc, pt * 128:(pt + 1) * 128], w3[2][:, kc, :],
                                    start=(kc == 0), stop=(kc == DC - 1)))
                        k.mm_group(fns, reads=[w3[2], xT], writes=[ps])
                        k.op(k.act, lambda e, ps=ps: e.activation(
                            out=V[hb][:, g4:g4 + n, :].rearrange("p a b -> p (a b)"), in_=ps[:, 0:n * 128], func=AF.Copy),
                            reads=[ps], writes=[V[hb]])
                    return f

                chunks = [cq(0), cq(1)] + [ck(s0, sn) for (s0, sn) in KS] + [cv(0), cv(4), cv(8)]
                return chunks

            def phaseA(i, h, m):
                hb = h % 2
                p0 = min(max(m - 2, 0), 5)
                k0 = p0 * 128
                b_, s_, p_, m_ = bt[i % 2], sbt[i % 2], Pt[i % 3], sm[i % 3]
                qs = slice(m * 128, (m + 1) * 128)
                k.dma(k.sp, b_[:, :], tb.t[h, m, :, :], reads=[tb], writes=[b_])
                for jj in range(2):
                    ps = psS[(i % 2) * 2 + jj]
                    k.mm_group([lambda e, jj=jj, ps=ps: e.matmul(
                        ps[:, 0:320], qT[hb][:, qs], kT[hb][:, k0 + jj * 320:k0 + (jj + 1) * 320],
                        start=True, stop=True)], reads=[qT[hb], kT[hb]], writes=[ps])
                    k.op(k.dve, lambda e, jj=jj, ps=ps: e.tensor_tensor(
                        out=s_[:, jj * 320:(jj + 1) * 320], in0=ps[:, 0:320], in1=b_[:, jj * 320:(jj + 1) * 320],
                        op=ALU.add), reads=[ps, b_], writes=[s_])
                k.op(k.dve, lambda e: e.tensor_reduce(out=m_[:, 3:4], in_=s_[:, :], axis=AX.X, op=ALU.max),
                     reads=[s_], writes=[m_])
                k.op(k.dve, lambda e: e.tensor_scalar(out=m_[:, 0:1], in0=m_[:, 3:4], scalar1=-1.0, scalar2=None, op0=ALU.mult),
                     reads=[m_], writes=[m_])
                k.op(k.act, lambda e: e.activation(out=p_[:, :], in_=s_[:, :], func=AF.Exp, bias=m_[:, 0:1],
                                                   accum_out=m_[:, 1:2]), reads=[s_, m_], writes=[p_, m_])
                k.op(k.dve, lambda e: e.reciprocal(out=m_[:, 2:3], in_=m_[:, 1:2]), reads=[m_], writes=[m_])

            def phaseB(i, h, m):
                p_, pT_, pst = Pt[i % 3], PT[i % 3], psT[i % 2]
                k.mm_group([lambda e, ii=ii: e.transpose(pst[:, ii * 128:(ii + 1) * 128], p_[:, ii * 128:(ii + 1) * 128], ident[:, :])
                            for ii in range(5)], reads=[p_, ident], writes=[pst])
                k.op(k.act, lambda e: e.activation(out=pT_[:, :, :].rearrange("p a b -> p (a b)"), in_=pst[:, 0:640],
                                                   func=AF.Copy), reads=[pst], writes=[pT_])

            def phaseC(i, h, m):
                hb = h % 2
                p0 = min(max(m - 2, 0), 5)
                pT_, o_, m_, pso, pt2 = PT[i % 3], Osb[i % 2], sm[i % 3], psO[i % 2], psT2[i % 2]
                qs = slice(m * 128, (m + 1) * 128)
                k.mm_group([lambda e, ii=ii: e.matmul(pso[:, 0:128], pT_[:, ii, :], V[hb][:, p0 + ii, :],
                                                      start=(ii == 0), stop=(ii == 4)) for ii in range(5)],
                           reads=[pT_, V[hb]], writes=[pso])
                k.op(k.dve, lambda e: e.tensor_scalar(out=o_[:, :], in0=pso[:, 0:128], scalar1=m_[:, 2:3], scalar2=None,
                                                      op0=ALU.mult), reads=[pso, m_], writes=[o_])
                k.mm_group([lambda e: e.transpose(pt2[:, 0:128], o_[:, :], ident[:, :])],
                           reads=[o_, ident], writes=[pt2])
                k.op(k.dve, lambda e: e.tensor_copy(out=OT[:, h, qs], in_=pt2[:, 0:128]),
                     reads=[pt2], writes=[OT])

            items = [(h, m) for h in range(16) for m in range(NM)]
            pending = {}
            for f in proj_chunks(0):
                f()
            n = len(items)
            for step in range(n + 2):
                if step < n:
                    h, m = items[step]
                    if m == 0 and h + 1 < 16:
                        pending = {"fs": proj_chunks(h + 1), "i": 0}
                    phaseA(step, h, m)
                    if pending and pending["i"] < len(pending["fs"]):
                        per = -(-len(pending["fs"]) // max(NM, 1))
                        for _ in range(per):
                            if pending["i"] < len(pending["fs"]):
                                pending["fs"][pending["i"]]()
                                pending["i"] += 1
                if 0 <= step - 1 < n:
                    phaseB(step - 1, *items[step - 1])
                if 0 <= step - 2 < n:
                    phaseC(step - 2, *items[step - 2])
            k.barrier()
        proj_resid(P, OT, DC, wout, xres, ybuf)


def ro(ap):
    b = Buf(ap)
    return b


WSHAPES = {"gla_w_in": (D, 6176), "gla_w_out": (D, D), "na_w_in": (D, 3 * D), "na_w_out": (D, D),
           "ffn_w_up": (D, 2 * FF), "ffn_w_down": (FF, D)}
SUB_W = {"gla": ("gla_w_in", "gla_w_out"), "na": ("na_w_in", "na_w_out"), "ffn": ("ffn_w_up", "ffn_w_down")}


def declare_weights(P, subs):
    P.wfull, P.wsh, P.wbounce = {}, {}, {}
    for kind, idx, li in subs:
        for base in SUB_W[kind]:
            name = "%s_%d" % (base, idx)
            Kr, N = WSHAPES[base]
            P.wsh[name] = P.din(name, [Kr // 8, N])
            P.wbounce[name] = P.dscr(name + "_b", [Kr // 8, N], BF16)
            P.wfull[name] = P.dscr(name + "_f", [Kr, N], BF16)


def gather_weights(P, kind, idx):
    k = P.k
    for base in SUB_W[kind]:
        name = "%s_%d" % (base, idx)
        sh, bo, fu = P.wsh[name], P.wbounce[name], P.wfull[name]
        Kr, N = WSHAPES[base]
        rows = Kr // 8
        v = "(a r) n -> a r n"
        nsp = 4
        for a in range(nsp):
            r0_, r1_ = a * rows // nsp, (a + 1) * rows // nsp
            k.dma(k.pool, bo.t[r0_:r1_, :], sh.t[r0_:r1_, :], reads=[sh], writes=[bo])
        k.acc(k.pool, [bo], [fu])
        ins = k.nc.gpsimd.collective_compute("AllGather", ALU.bypass, replica_groups=[list(range(8))],
                                              ins=[bo.th.ap().opt()], outs=[fu.th.ap().opt()])
        P.xch["cnt"] += 1
        ins.then_inc(P.xch["sem"])
        k.done((P.xch["sem"], P.xch["cnt"]), [bo], [fu])


def make_consts(P, st):
    k = P.k
    c = {}
    ones = k.sb(st, [128, 128], F32, "ones")
    k.op(k.pool, lambda e: e.memset(ones[:, :], 1.0), writes=[ones])
    eps = k.sb(st, [128, 2], F32, "eps")
    k.op(k.pool, lambda e: e.memset(eps[:, 0:1], LN_EPS), writes=[eps])
    k.op(k.pool, lambda e: e.memset(eps[:, 1:2], RMS_EPS), writes=[eps])
    c["ones"] = ones
    c["eps_ln"] = eps
    flags = k.sb(st, [128, 2], F32, "flags")
    k.dma(k.sp, flags[:, :], P.dram_in["flags"].t, writes=[flags])
    c["flags"] = flags
    gb = k.sb(st, [128, 2 * DEPTH * 2 * DC], F32, "gb")
    k.dma(k.sp, gb[:, :], P.dram_in["gb"].t, writes=[gb])
    c["gb"] = gb
    cv = k.sb(st, [128, DEPTH * 88 * 4], F32, "cv")
    k.dma(k.sp, cv[:, :], P.dram_in["cv"].t, writes=[cv])
    c["cv"] = cv
    idf = k.sb(st, [128, 128], F32, "idf")
    k.dma(k.sp, idf[:, :], P.dram_in["ident"].t, writes=[idf])
    idb = k.sb(st, [128, 128], BF16, "idb")
    k.op(k.dve, lambda e: e.tensor_copy(out=idb[:, :], in_=idf[:, :]), reads=[idf], writes=[idb])
    c["ident_bf"] = idb
    c["ident_f"] = idf
    return c


def build(cfg):
    P = Prog(cfg)
    k = P.k
    subs = cfg["subs"]
    kinds = set(s[0] for s in subs)
    x_in = P.din("xT0", [D, TL])
    P.din("flags", [128, 2])
    P.din("gb", [128, 2 * DEPTH * 2 * DC])
    P.din("cv", [128, DEPTH * 88 * 4])
    P.din("ident", [128, 128])
    declare_weights(P, subs)
    P.gla_params = {}
    P.na_params = {}
    for kind, idx, li in subs:
        if kind == "gla":
            P.gla_params[idx] = {"wg": P.din("gla_wg_%d" % idx, [16, 2, 1024]),
                                 "bg": P.din("gla_bg_%d" % idx, [128, 16]),
                                 "ng": P.din("gla_ng_%d" % idx, [128, 512])}
        if kind == "na":
            P.na_params[idx] = {"tb": P.din("na_tb_%d" % idx, [16, 8, 128, 640])}
    if "gla" in kinds:
        P.din("gla_masks", [128, 2, 128])
        P.gla_o1 = P.dscr("gla_o1", [TL, D])
        P.gla_snd = P.dscr("gla_snd", [1024, 512])
        P.gla_rcv = P.dscr("gla_rcv", [2048, 512])
    out = P.dout("outT", [D, TL])
    ybuf = P.dscr("ybuf", [D, TL])
    xresA = P.dscr("xresA", [D, TL])
    xresB = P.dscr("xresB", [D, TL])
    xch = {"snd": P.dscr("xsnd", [D, 256], BF16), "rcv": P.dscr("xrcv", [2 * D, 256], BF16),
           "sem": k.new_sem("cc"), "cnt": 0}
    P.xch = xch
    with ExitStack() as st:
        consts = make_consts(P, st)
        xT = k.sb(st, [128, DC, XW], BF16, "xT")
        halo_of = {"ffn": "ffn", "na": "na", "gla": None}
        gather_weights(P, subs[0][0], subs[0][1])
        with ExitStack() as st1:
            load_xT(P, st1, x_in, xT)
            if halo_of[subs[0][0]] is not None:
                exchange_halo(P, st1, xT, halo_of[subs[0][0]], xch, consts)
            k.barrier()
        xres = x_in
        for si, (kind, idx, li) in enumerate(subs):
            last = si == len(subs) - 1
            xres_out = out if last else (xresA if si % 2 == 0 else xresB)
            nxt = None if last else halo_of[subs[si + 1][0]]
            if not last:
                gather_weights(P, subs[si + 1][0], subs[si + 1][1])
            if kind == "ffn":
                ffn_stage(P, xT, P.wfull["ffn_w_up_%d" % idx], P.wfull["ffn_w_down_%d" % idx], consts["cv"], idx, xres, ybuf)
            elif kind == "gla":
                gla_stage(P, xT, idx, xres, ybuf, consts)
            elif kind == "na":
                na_stage(P, xT, idx, xres, ybuf, consts)
            ln_stage(P, ybuf, consts["gb"], li, xres_out, xT, consts, nxt, xch)
            xres = xres_out
        k.barrier()
    k.es.close()
    return P


def na_bias_table(rpb, flip):
    NEG = np.float32(-1e9)
    H = rpb.shape[0]
    tab = np.full((H, 64, 16, 640), NEG, np.float32)
    rows_l = np.arange(16)
    for r in range(16):
        p0 = min(max(r // 2 - 2, 0), 5)
        for i in range(10):
            e = 2 * p0 + i
            if e < 16:
                kr_loc, same = e, True
            else:
                kr_loc, same = 15 - (e - 16), False
            for c in range(64):
                if not flip:
                    gq_r, gq_c = r, c
                    if same:
                        gk_r = kr_loc
                        gk_c = np.arange(64)
                    else:
                        gk_r = 31 - kr_loc
                        gk_c = 63 - np.arange(64)
                else:
                    gq_r, gq_c = 31 - r, 63 - c
                    if same:
                        gk_r = 31 - kr_loc
                        gk_c = 63 - np.arange(64)
                    else:
                        gk_r = kr_loc
                        gk_c = np.arange(64)
                rs = min(max(gq_r - 4, 0), 24)
                if not (rs <= gk_r < rs + 8):
                    continue
                wc = min(max(gq_c - 8, 0), 48)
                valid = (gk_c >= wc) & (gk_c < wc + 16)
                dr = gk_r - gq_r + 7
                dc = np.clip(gk_c - gq_c, -15, 15) + 15
                vals = rpb[:, dr, :][:, dc]
                blk = tab[:, c, r, i * 64:(i + 1) * 64]
                blk[:, valid] = vals[:, valid]
    return tab


def host_inputs(inp, subs):
    x = np.asarray(inp["x"], np.float32)
    maps = []
    gbcols = []
    for layer in range(DEPTH):
        for g_, b_ in ((inp["ln_mix_g"], inp["ln_mix_b"]), (inp["ln_ffn_g"], inp["ln_ffn_b"])):
            gbcols.append(np.asarray(g_[layer], np.float32).reshape(DC, 128).T)
            gbcols.append(np.asarray(b_[layer], np.float32).reshape(DC, 128).T)
    gb = np.ascontiguousarray(np.concatenate(gbcols, axis=1))
    cw = np.asarray(inp["ffn_conv_w"], np.float32)
    cb = np.asarray(inp["ffn_conv_b"], np.float32)

    def conv_tab(flip):
        w = cw[:, ::-1, :] if flip else cw
        t = np.concatenate([w, cb[:, None, :]], axis=1)
        t = t.reshape(DEPTH, 4, 88, 128).transpose(3, 0, 2, 1)
        return np.ascontiguousarray(t.reshape(128, DEPTH * 88 * 4))

    cvs = [conv_tab(False), conv_tab(True)]
    ident = np.eye(128, dtype=np.float32)
    tri_le = np.triu(np.ones((128, 128), np.float32))
    tri_lt = np.triu(np.ones((128, 128), np.float32), 1)
    gmask = [np.ascontiguousarray(np.stack([tri_le, tri_lt.T], axis=1)),
             np.ascontiguousarray(np.stack([tri_lt, tri_le.T], axis=1))]
    per_half = [{}, {}]
    for kind, idx, li in subs:
        if kind == "gla":
            wgu = np.asarray(inp["gla_w_gate_up"][idx], np.float32)
            bgt = np.asarray(inp["gla_b_gate"][idx], np.float32)
            ngt = np.asarray(inp["gla_norm_g"][idx], np.float32)
            for h in range(2):
                zs = [0, 1] if h == 0 else [1, 0]
                per_half[h]["gla_wg_%d" % idx] = np.ascontiguousarray(wgu[zs].transpose(1, 0, 2))
                per_half[h]["gla_bg_%d" % idx] = np.ascontiguousarray(
                    bgt[zs].reshape(2, 8, 128).transpose(2, 0, 1).reshape(128, 16))
                per_half[h]["gla_ng_%d" % idx] = np.ascontiguousarray(np.tile(ngt[None, :], (128, 1)))
        if kind == "na":
            rpb = np.asarray(inp["na_rpb"][idx], np.float32)
            for h in range(2):
                tab = na_bias_table(rpb, h == 1)
                per_half[h]["na_tb_%d" % idx] = np.ascontiguousarray(tab.transpose(0, 2, 1, 3).reshape(16, 8, 128, 640))
    for c in range(8):
        b, h = c // 2, c % 2
        xs = x[b, h * TL:(h + 1) * TL, :]
        if h == 1:
            xs = xs[::-1]
        m = {"xT0": np.ascontiguousarray(xs.T),
             "flags": np.ascontiguousarray(np.tile(np.array([[1.0 - h, float(h)]], np.float32), (128, 1))),
             "gb": gb, "cv": cvs[h], "ident": ident}
        if any(s_[0] == "gla" for s_ in subs):
            m["gla_masks"] = gmask[h]
        m.update(per_half[h])
        for kind, idx, li in subs:
            for base in SUB_W[kind]:
                Kr, N = WSHAPES[base]
                rows = Kr // 8
                m["%s_%d" % (base, idx)] = np.ascontiguousarray(np.asarray(inp[base][idx], np.float32)[c * rows:(c + 1) * rows])
        maps.append(m)
    return maps


def host_output(res):
    out = np.empty((4, 2 * TL, D), np.float32)
    for c in range(8):
        b, h = c // 2, c % 2
        o = res[c]["outT"].T
        if h == 1:
            o = o[::-1]
        out[b, h * TL:(h + 1) * TL, :] = o
    return out


FULL_SUBS = [("gla", 0, 0), ("ffn", 0, 1), ("na", 0, 2), ("ffn", 1, 3),
             ("gla", 1, 4), ("ffn", 2, 5), ("na", 1, 6), ("ffn", 3, 7)]


def run(inp, subs, **cfg):
    P = build(dict(cfg, subs=subs))
    maps = host_inputs(inp, subs)
    res = run_bass_kernel_spmd(P.nc, maps, core_ids=list(range(8)))
    return host_output(res.results)


def kernel(**inputs):
    return run(inputs, FULL_SUBS)
```
